# Optimizing a Trainium2 kernel written in Bass

```python
import jax, jax.numpy as jnp
from jax import lax
import numpy as np

D_MODEL = 2048
BATCH = 4
SEQ = 2048
DEPTH = 1

CHUNK = 64
N_META = 16
D_MIX = D_MODEL
D_POOL = D_MIX // 2
POOL_WINDOWS = (2, 4, 8, 16)
N_POOL_GROUPS = len(POOL_WINDOWS)
POOL_GROUP = D_POOL // N_POOL_GROUPS
D_ATT = D_MIX - D_POOL
HEAD_DIM = 128
N_HEADS = D_ATT // HEAD_DIM
D_IN = D_POOL + 3 * D_ATT + N_HEADS
D_FF = ((8 * D_MODEL // 3 + 255) // 256) * 256
Q_BLOCK = 128
EPS = 1e-6

kernel_name = "hymba_pool_fox_macaron_block"


def rmsnorm(x, g):
    xf = x.astype(jnp.float32)
    y = xf * lax.rsqrt(jnp.mean(xf * xf, axis=-1, keepdims=True) + EPS)
    return (y * g.astype(jnp.float32)).astype(x.dtype)


def swiglu(x, w_gate, w_up, w_down):
    return (jax.nn.silu(x @ w_gate) * (x @ w_up)) @ w_down


def pool_mixer(p, pool_w, pool_scale):
    B, L, _ = p.shape
    pg = p.reshape(B, L, N_POOL_GROUPS, POOL_GROUP)
    c = jnp.cumsum(pg.astype(jnp.float32), axis=1)
    c = jnp.pad(c, ((0, 0), (1, 0), (0, 0), (0, 0)))
    win = jnp.array(POOL_WINDOWS, dtype=jnp.int32)
    end = jnp.arange(1, L + 1, dtype=jnp.int32)[:, None]
    start = jnp.maximum(end - win[None, :], 0)
    gidx = jnp.arange(N_POOL_GROUPS, dtype=jnp.int32)[None, :]
    window_sum = c[:, end, gidx] - c[:, start, gidx]
    count = (end - start).astype(jnp.float32)[None, :, :, None]
    pooled = (window_sum / count - pg.astype(jnp.float32)).astype(p.dtype)
    mixed = jnp.einsum('blgc,gcd->blgd', pooled, pool_w)
    return mixed.reshape(B, L, D_POOL) * pool_scale


def fox_attention(q, k, v, log_f):
    B, L, H, Dh = q.shape
    scale = 1.0 / np.sqrt(Dh).astype(np.float32)
    cum = jnp.cumsum(log_f, axis=-1)
    n_blocks = -(-L // Q_BLOCK)
    Lp = n_blocks * Q_BLOCK
    qp = jnp.pad(q, ((0, 0), (0, Lp - L), (0, 0), (0, 0)))
    cqp = jnp.pad(cum, ((0, 0), (0, 0), (0, Lp - L)))
    qb = qp.reshape(B, n_blocks, Q_BLOCK, H, Dh).transpose(1, 0, 2, 3, 4)
    cqb = cqp.reshape(B, H, n_blocks, Q_BLOCK).transpose(2, 0, 1, 3)
    kpos = jnp.arange(L, dtype=jnp.int32)

    def one_block(args):
        qi, cqi, bi = args
        qpos = bi * Q_BLOCK + jnp.arange(Q_BLOCK, dtype=jnp.int32)
        s = jnp.einsum('bqhd,bkhd->bhqk', qi, k).astype(jnp.float32) * scale
        s = s + (cqi[:, :, :, None] - cum[:, :, None, :])
        s = jnp.where(qpos[:, None] >= kpos[None, :], s, -jnp.inf)
        pr = jax.nn.softmax(s, axis=-1)
        return jnp.einsum('bhqk,bkhd->bqhd', pr.astype(v.dtype), v)

    out = lax.map(one_block, (qb, cqb, jnp.arange(n_blocks, dtype=jnp.int32)))
    out = out.transpose(1, 0, 2, 3, 4).reshape(B, Lp, H * Dh)
    return out[:, :L]


def setup_inputs(seed: int = 0) -> dict:
    key = jax.random.key(seed)
    ks = jax.random.split(key, 20)
    f32 = jnp.float32

    def nrm(k, shape, s):
        return jax.random.normal(k, shape, f32) * s

    def gain(k, shape):
        return 1.0 + 0.05 * jax.random.normal(k, shape, f32)

    return {
        "x": jax.random.normal(ks[0], (BATCH, SEQ, D_MODEL), f32),
        "meta_tokens": nrm(ks[1], (N_META, D_MODEL), 1.0),
        "ffn1_norm": gain(ks[2], (DEPTH, D_MODEL)),
        "ffn1_w_gate": nrm(ks[3], (DEPTH, D_MODEL, D_FF), D_MODEL ** -0.5),
        "ffn1_w_up": nrm(ks[4], (DEPTH, D_MODEL, D_FF), D_MODEL ** -0.5),
        "ffn1_w_down": nrm(ks[5], (DEPTH, D_FF, D_MODEL), D_FF ** -0.5),
        "mix_norm": gain(ks[6], (DEPTH, D_MODEL)),
        "w_in": nrm(ks[7], (DEPTH, D_MODEL, D_IN), D_MODEL ** -0.5),
        "b_forget": jax.random.uniform(ks[8], (DEPTH, N_HEADS), f32, minval=1.0, maxval=5.0),
        "q_norm": gain(ks[9], (DEPTH, HEAD_DIM)),
        "k_norm": gain(ks[10], (DEPTH, HEAD_DIM)),
        "pool_w": nrm(ks[11], (DEPTH, N_POOL_GROUPS, POOL_GROUP, POOL_GROUP), POOL_GROUP ** -0.5),
        "pool_scale": 1.0 + 0.1 * jax.random.normal(ks[12], (DEPTH, D_POOL), f32),
        "w_out": nrm(ks[13], (DEPTH, D_MIX, D_MODEL), D_MIX ** -0.5),
        "ffn2_norm": gain(ks[14], (DEPTH, D_MODEL)),
        "ffn2_w_gate": nrm(ks[15], (DEPTH, D_MODEL, D_FF), D_MODEL ** -0.5),
        "ffn2_w_up": nrm(ks[16], (DEPTH, D_MODEL, D_FF), D_MODEL ** -0.5),
        "ffn2_w_down": nrm(ks[17], (DEPTH, D_FF, D_MODEL), D_FF ** -0.5),
    }


def reference(x, meta_tokens, ffn1_norm, ffn1_w_gate, ffn1_w_up, ffn1_w_down, mix_norm, w_in,
              b_forget, q_norm, k_norm, pool_w, pool_scale, w_out, ffn2_norm, ffn2_w_gate,
              ffn2_w_up, ffn2_w_down):
    B = x.shape[0]
    meta = jnp.broadcast_to(meta_tokens[None].astype(x.dtype), (B, N_META, D_MODEL))
    h = jnp.concatenate([meta, x], axis=1)
    L = h.shape[1]
    for i in range(DEPTH):
        h = h + 0.5 * swiglu(rmsnorm(h, ffn1_norm[i]), ffn1_w_gate[i], ffn1_w_up[i], ffn1_w_down[i])

        u = rmsnorm(h, mix_norm[i])
        z = u @ w_in[i]
        o = D_POOL
        p = z[..., :o]
        q = z[..., o:o + D_ATT].reshape(B, L, N_HEADS, HEAD_DIM)
        k = z[..., o + D_ATT:o + 2 * D_ATT].reshape(B, L, N_HEADS, HEAD_DIM)
        v = z[..., o + 2 * D_ATT:o + 3 * D_ATT].reshape(B, L, N_HEADS, HEAD_DIM)
        f_logit = z[..., o + 3 * D_ATT:]

        pool_out = pool_mixer(p, pool_w[i], pool_scale[i])

        q = rmsnorm(q, q_norm[i])
        k = rmsnorm(k, k_norm[i])
        log_f = jax.nn.log_sigmoid(f_logit.astype(jnp.float32) + b_forget[i].astype(jnp.float32))
        att_out = fox_attention(q, k, v, log_f.transpose(0, 2, 1))

        mix = jnp.concatenate([pool_out, att_out.astype(pool_out.dtype)], axis=-1)
        h = h + mix @ w_out[i]

        h = h + 0.5 * swiglu(rmsnorm(h, ffn2_norm[i]), ffn2_w_gate[i], ffn2_w_up[i], ffn2_w_down[i])
    return h[:, N_META:]
```

```python
import os
import numpy as np
import concourse.bass as bass
import concourse.mybir as mybir
from concourse.bass_utils import run_bass_kernel_spmd

F32 = mybir.dt.float32
BF16 = mybir.dt.bfloat16
ALU = mybir.AluOpType
AF = mybir.ActivationFunctionType

D = 2048
DC = 16
FF = 5632
FC = 44
NG = 4
GJ = 11
T = 1024
NE = 32
TX = T + NE
NH = 8
EPS = 1e-6
NEG = -30000.0
ATT_SCALE = float(1.0 / np.sqrt(128.0))

ENG = ("pe", "act", "dve", "pool", "sp")


class Tok:
    __slots__ = ("key", "sem", "val")

    def __init__(self, key, sem, val):
        self.key, self.sem, self.val = key, sem, val


class Sched:
    def __init__(self, esem):
        self.q = {e: [] for e in ENG}
        self.esem = esem
        self.ecnt = {e: 0 for e in ENG}
        self.waited = {e: {} for e in ENG}
        self.lastw = {}
        self.lastr = {}
        self.dcnt = {}
        self.dsem = {}
        self.bar = []

    def _deps(self, eng, reads, writes, extra):
        need = {}

        def add(t):
            if t is None:
                return
            if t.key == eng and eng == "pe":
                return
            o = need.get(t.key)
            if o is None or o.val < t.val:
                need[t.key] = t

        for k in reads:
            add(self.lastw.get(k))
        for k in writes:
            add(self.lastw.get(k))
            for t in self.lastr.get(k, {}).values():
                add(t)
        for t in extra:
            add(t)
        for t in self.bar:
            add(t)
        waits = []
        for key, t in need.items():
            if self.waited[eng].get(key, 0) < t.val:
                self.waited[eng][key] = t.val
                waits.append((t.sem, t.val))
        return waits

    def _record(self, tok, reads, writes):
        for k in reads:
            d = self.lastr.setdefault(k, {})
            o = d.get(tok.key)
            if o is None or o.val < tok.val:
                d[tok.key] = tok
        for k in writes:
            self.lastw[k] = tok
            self.lastr[k] = {}

    def op(self, eng, fn, reads=(), writes=(), sig=True, extra=()):
        sig = True
        waits = self._deps(eng, reads, writes, extra)
        if sig:
            self.ecnt[eng] += 1
            tok = Tok(eng, self.esem[eng], self.ecnt[eng])
        else:
            tok = Tok(eng, self.esem[eng], self.ecnt[eng] + 1)
        self.q[eng].append((waits, fn, 1 if sig else 0, None))
        self._record(tok, reads, writes)
        return tok

    def dma(self, eng, fn, sem, reads=(), writes=(), extra=(), inc=16):
        waits = self._deps(eng, reads, writes, extra)
        k = id(sem)
        self.dcnt[k] = self.dcnt.get(k, 0) + inc
        self.dsem[k] = sem
        tok = Tok(("d", k), sem, self.dcnt[k])
        self.q[eng].append((waits, fn, 2, (sem, inc)))
        self._record(tok, reads, writes)
        return tok

    def barrier(self):
        b = []
        for e in ENG:
            if self.ecnt[e] > 0:
                b.append(Tok(e, self.esem[e], self.ecnt[e]))
        for k, c in self.dcnt.items():
            b.append(Tok(("d", k), self.dsem[k], c))
        self.bar = b

    def retire(self, keys):
        toks = []
        for k in keys:
            t = self.lastw.get(k)
            if t is not None:
                toks.append(t)
            toks.extend(self.lastr.get(k, {}).values())
        return toks

    def inherit(self, keys, toks):
        for k in keys:
            d = self.lastr.setdefault(k, {})
            for t in toks:
                o = d.get(t.key)
                if o is None or o.val < t.val:
                    d[t.key] = t

    def replay(self, eng, engine):
        for waits, fn, kind, info in self.q[eng]:
            for sem, val in waits:
                engine.wait_ge(sem, val)
            ins = fn(engine)
            if kind == 1:
                ins.then_inc(self.esem[eng], 1)
            elif kind == 2:
                if info[1] == 16:
                    ins.then_inc(info[0], 16)
                else:
                    ins.then_inc(info[0])


def v3(ap, a):
    return ap.rearrange("p (a b) -> p a b", a=a)


STAGE = int(os.environ.get("MK_STAGE", "9"))


def build_program():
    nc = bass.Bass("TRN2", target_bir_lowering=False)

    def din(name, shape, dt=F32):
        return nc.dram_tensor(name, list(shape), dt, kind="ExternalInput").ap()

    xT_d = din("xT", [D, T])
    xe_d = din("xeT", [D, NE])
    gains_d = din("gains", [128, 48])
    qkg_d = din("qkg", [128, 2])
    pscale_d = din("pscale", [128, 8])
    bfr_d = din("bfr", [128, 8])
    core_d = din("corec", [128, 4])
    identf_d = din("identf", [128, 128])
    onesf_d = din("onesf", [128, 128])
    tri_d = din("tri", [128, 128])
    cb_d = din("cbf", [128, 2304])
    esel_d = din("esel", [128, 1024])
    wgu_d = [din("wgu1", [FC, 128, 4096]), din("wgu2", [FC, 128, 4096])]
    wd_d = [din("wd1", [NG * 8, 128, GJ * 256]), din("wd2", [NG * 8, 128, GJ * 256])]
    win_d = din("win", [32, 128, 2048])
    wf_d = din("wf", [128, 128])
    wout_d = din("wout", [16, 128, 2048])
    poolw_d = din("poolw", [128, 2048])
    out_d = nc.dram_tensor("outT", [D, T], F32, kind="ExternalOutput").ap()

    kb_d = nc.dram_tensor("kb", [1024, T], BF16)
    vb_d = nc.dram_tensor("vb", [1024, 1024], BF16)
    lb_d = nc.dram_tensor("lb", [1024, 8], F32)
    kg_d = nc.dram_tensor("kg", [2048, T], BF16)
    vg_d = nc.dram_tensor("vg", [2048, 1024], BF16)
    lg_d = nc.dram_tensor("lg", [2048, 8], F32)

    def sb(name, cols, dt):
        return nc.alloc_sbuf_tensor("sb_" + name, [128, cols], dt)

    hT = sb("hT", DC * T, F32)
    hE = sb("hE", DC * NE, F32)
    xn = sb("xn", DC * TX, BF16)
    identf = sb("identf", 128, F32)
    onesf = sb("onesf", 128, F32)
    tri = sb("tri", 128, F32)
    cb = sb("cb", 2304, BF16)
    esel = sb("esel", 1024, BF16)
    gains = sb("gains", 48, F32)
    qkg = sb("qkg", 2, F32)
    pscale = sb("pscale", 8, F32)
    bfr = sb("bfr", 8, F32)
    corec = sb("corec", 4, F32)
    poolw = sb("poolw", 2048, BF16)
    wf = sb("wf", 128, BF16)
    rstd = sb("rstd", TX, F32)
    sq = sb("sq", 3 * 512, BF16)
    PH = 43008
    ph = sb("ph", PH, BF16)

    identb = cb[:, 0:128]
    onesb = cb[:, 128:256]

    def diag(i):
        return cb[:, 256 + i * 512:256 + (i + 1) * 512]

    o = 0
    WGU_OFF = [o + i * 4096 for i in range(3)]; o += 3 * 4096
    WD_OFF = [o + i * 2816 for i in range(3)]; o += 3 * 2816
    SILU_OFF = [o + i * 1024 for i in range(3)]; o += 3 * 1024
    ACT_OFF = o; o += GJ * TX
    assert o <= PH, o
    o = 0
    WIN_OFF = [o + i * 2048 for i in range(3)]; o += 3 * 2048
    MISC_OFF = o; o += 8192
    QT_OFF = o; o += 8320
    KT_OFF = o; o += 8 * 1040
    V_OFF = o; o += 9 * 1024
    SM_OFF = o; o += 2560
    assert o <= PH, o

    def phb(off, n):
        return ph[:, off:off + n]

    def phf(off, n):
        return ph[:, off:off + 2 * n].bitcast(F32)

    pst = [nc.alloc_psum_tensor(f"ps{i}", [128, 512], F32) for i in range(8)]

    esem = {e: nc.alloc_semaphore(name=f"s_{e}") for e in ENG}
    S = Sched(esem)
    nsem = [0]

    def newsem():
        nsem[0] += 1
        return nc.alloc_semaphore(name=f"d{nsem[0]}")

    free_banks = list(range(8))

    def acq():
        return free_banks.pop(0)

    def rel(b):
        free_banks.append(b)

    def PK(b):
        return ("ps", b)

    def dma_in(eng, dst, src, key, sem=None):
        sem = sem or newsem()
        return S.dma(eng, lambda e, dst=dst, src=src: e.dma_start(out=dst, in_=src), sem, writes=[key])

    def mm(out, lhsT, rhs, start, stop, reads, bank, sig=False):
        return S.op("pe", lambda e: e.matmul(out, lhsT, rhs, start=start, stop=stop),
                    reads=reads, writes=[PK(bank)], sig=sig)

    for bi in range(2):
        dma_in("sp", v3(hT[:, :], DC)[:, :, bi * 512:(bi + 1) * 512],
               xT_d[:, bi * 512:(bi + 1) * 512].rearrange("(c p) t -> p c t", p=128), ("h", bi))
    dma_in("sp", v3(hE[:, :], DC), xe_d.rearrange("(c p) t -> p c t", p=128), "hE")
    for dst, src, key in ((gains, gains_d, "gains"), (qkg, qkg_d, "qkg"), (pscale, pscale_d, "pscale"),
                          (bfr, bfr_d, "bfr"), (corec, core_d, "corec"), (identf, identf_d, "identf"),
                          (onesf, onesf_d, "onesf"), (tri, tri_d, "tri")):
        dma_in("sp", dst[:, :], src, key)
    dma_in("pool", cb[:, :], cb_d, "cb")
    dma_in("pool", esel[:, :], esel_d, "esel")
    dma_in("pool", poolw[:, :], poolw_d, "poolw")
    dma_in("pool", wf[:, :], wf_d, "wf")

    def hkeys_for(c, bi):
        return [("h", bi), ("hc", c)]

    def own_src(c, c0, n):
        return hT[:, c * T + c0:c * T + c0 + n]

    B0 = dict(name="b0", c0=0, n=512, src=lambda c: own_src(c, 0, 512), hk=lambda c: hkeys_for(c, 0))
    B1 = dict(name="b1", c0=512, n=512, src=lambda c: own_src(c, 512, 512), hk=lambda c: hkeys_for(c, 1))
    BE = dict(name="be", c0=1024, n=NE, src=lambda c: hE[:, c * NE:(c + 1) * NE], hk=lambda c: ["hE", ("hEc", c)])

    def xn_ap(c, blk):
        return xn[:, c * TX + blk["c0"]:c * TX + blk["c0"] + blk["n"]]

    def xnk(c, blk):
        return ("xn", c, blk["name"])

    sqrot = [0]

    def norm(blocks, gcol):
        banks = {blk["name"]: acq() for blk in blocks}
        for blk in blocks:
            n = blk["n"]
            bank = banks[blk["name"]]
            for c in range(DC):
                sl = sqrot[0] % 3
                sqrot[0] += 1
                sqa = sq[:, sl * 512:sl * 512 + n]
                S.op("act", lambda e, sqa=sqa, src=blk["src"](c): e.activation(out=sqa, in_=src, func=AF.Square),
                     reads=blk["hk"](c), writes=[("sq", sl)])
                mm(pst[bank][:, 0:n], onesb, sqa, c == 0, c == DC - 1, [("sq", sl), "cb"], bank, sig=(c == DC - 1))
        for blk in blocks:
            n = blk["n"]
            bank = banks[blk["name"]]
            ra = rstd[:, blk["c0"]:blk["c0"] + n]
            S.op("act", lambda e, ra=ra, pa=pst[bank][:, 0:n]: e.activation(
                out=ra, in_=pa, func=AF.Ln, bias=corec[:, 2:3], scale=1.0 / D),
                reads=[PK(bank), "corec"], writes=[("rstd", blk["name"])])
            rel(bank)
        for blk in blocks:
            ra = rstd[:, blk["c0"]:blk["c0"] + blk["n"]]
            S.op("act", lambda e, ra=ra: e.activation(out=ra, in_=ra, func=AF.Exp, scale=-0.5),
                 reads=[("rstd", blk["name"])], writes=[("rstd", blk["name"])])
        for c in range(DC):
            for blk in blocks:
                ra = rstd[:, blk["c0"]:blk["c0"] + blk["n"]]
                S.op("dve", lambda e, o_=xn_ap(c, blk), src=blk["src"](c), g=gains[:, gcol + c:gcol + c + 1], ra=ra:
                     e.scalar_tensor_tensor(out=o_, in0=src, scalar=g, in1=ra, op0=ALU.mult, op1=ALU.mult),
                     reads=blk["hk"](c) + [("rstd", blk["name"]), "gains"], writes=[xnk(c, blk)])

    silurot = [0]

    def ffn(fi, blocks, gcol, final_store=False):
        wgu_src = wgu_d[fi]
        wd_src = wd_d[fi]
        wsem = [newsem() for _ in range(3)]
        dsem = [newsem() for _ in range(3)]

        def load_wgu(j):
            sl = j % 3
            S.dma("pool", lambda e, sl=sl, j=j: e.dma_start(out=phb(WGU_OFF[sl], 4096), in_=wgu_src[j]),
                  wsem[sl], writes=[("wgu", sl)])

        def load_wd(n):
            sl = n % 3
            S.dma("pool", lambda e, sl=sl, n=n: e.dma_start(out=phb(WD_OFF[sl], 2816), in_=wd_src[n]),
                  dsem[sl], writes=[("wd", sl)])

        for j in range(3):
            load_wgu(j)
        for n in range(3):
            load_wd(n)
        norm(blocks, gcol)
        own = [b for b in blocks if b["name"] != "be"]
        has_e = len(own) != len(blocks)
        for G in range(NG):
            for jj in range(GJ):
                j = G * GJ + jj
                sl = j % 3
                w = phb(WGU_OFF[sl], 4096)
                gb = {b["name"]: acq() for b in own}
                eb = acq() if has_e else None
                ub = {b["name"]: acq() for b in own}
                for half, banks, ecol in ((0, gb, 0), (1, ub, 64)):
                    for k in range(DC):
                        lhsT = w[:, half * 2048 + k * 128:half * 2048 + (k + 1) * 128]
                        for b in blocks:
                            if b["name"] == "be":
                                bank = eb
                                outp = pst[eb][:, ecol:ecol + NE]
                            else:
                                bank = banks[b["name"]]
                                outp = pst[bank][:, 0:b["n"]]
                            mm(outp, lhsT, xn_ap(k, b), k == 0, k == DC - 1,
                               [("wgu", sl), xnk(k, b)], bank, sig=(k == DC - 1))
                for b in blocks:
                    n = b["n"]
                    ss = silurot[0] % 3
                    silurot[0] += 1
                    tmp = phf(SILU_OFF[ss], 512)[:, 0:n]
                    if b["name"] == "be":
                        gsrc, usrc, gk, uk = pst[eb][:, 0:NE], pst[eb][:, 64:64 + NE], eb, eb
                    else:
                        gk, uk = gb[b["name"]], ub[b["name"]]
                        gsrc, usrc = pst[gk][:, 0:n], pst[uk][:, 0:n]
                    S.op("act", lambda e, tmp=tmp, gsrc=gsrc: e.activation(out=tmp, in_=gsrc, func=AF.Silu),
                         reads=[PK(gk)], writes=[("silu", ss)])
                    dst = phb(ACT_OFF + jj * TX + b["c0"], n)
                    S.op("dve", lambda e, dst=dst, tmp=tmp, usrc=usrc: e.tensor_tensor(
                        out=dst, in0=tmp, in1=usrc, op=ALU.mult),
                        reads=[("silu", ss), PK(uk)], writes=[("act", jj, b["name"])])
                for b in own:
                    rel(gb[b["name"]])
                if has_e:
                    rel(eb)
                for b in own:
                    rel(ub[b["name"]])
                if j + 3 < FC:
                    load_wgu(j + 3)
            for cp in range(8):
                n_ = G * 8 + cp
                sl = n_ % 3
                w = phb(WD_OFF[sl], 2816)
                for ci in range(2):
                    c = cp * 2 + ci
                    banks = {b["name"]: acq() for b in blocks}
                    for jj in range(GJ):
                        lhsT = w[:, jj * 256 + ci * 128:jj * 256 + (ci + 1) * 128]
                        for b in blocks:
                            bank = banks[b["name"]]
                            mm(pst[bank][:, 0:b["n"]], lhsT, phb(ACT_OFF + jj * TX + b["c0"], b["n"]),
                               jj == 0, jj == GJ - 1, [("wd", sl), ("act", jj, b["name"])], bank,
                               sig=(jj == GJ - 1))
                    for b in blocks:
                        bank = banks[b["name"]]
                        hsrc = b["src"](c)
                        hk = ("hc", c) if b["name"] != "be" else ("hEc", c)
                        S.op("dve", lambda e, hsrc=hsrc, pa=pst[bank][:, 0:b["n"]]: e.scalar_tensor_tensor(
                            out=hsrc, in0=pa, scalar=0.5, in1=hsrc, op0=ALU.mult, op1=ALU.add),
                            reads=[PK(bank)] + b["hk"](c), writes=[hk])
                        rel(bank)
                    if final_store and G == NG - 1 and c % 4 == 3:
                        c4 = c // 4
                        out_toks.append(S.dma(
                            "sp", lambda e, c4=c4: e.dma_start(
                                out=out_d[c4 * 512:(c4 + 1) * 512, :].rearrange("(c p) t -> p c t", p=128),
                                in_=v3(hT[:, c4 * 4 * T:(c4 + 1) * 4 * T], 4)),
                            newsem(), reads=[("hc", cc) for cc in range(c4 * 4, c4 * 4 + 4)] + [("h", 0), ("h", 1)]))
                if n_ + 3 < NG * 8:
                    load_wd(n_ + 3)

    out_toks = []

    if STAGE >= 1:
        ffn(0, [B0, B1, BE], 0)
        S.barrier()
    elif STAGE == 0:
        norm([B0, B1, BE], 0)

    def mixer():
        winsem = [newsem() for _ in range(3)]
        items = [("p", m) for m in range(8)] + [("k", m) for m in range(8)] + [("v", m) for m in range(8)] \
            + [("q", m) for m in range(8)] + [("o", c) for c in range(16)]
        colbase = {"p": 0, "q": 8, "k": 16, "v": 24}

        def load_item(i):
            kind, m = items[i]
            sl = i % 3
            src = wout_d[m] if kind == "o" else win_d[colbase[kind] + m]
            S.dma("pool", lambda e, sl=sl, src=src: e.dma_start(out=phb(WIN_OFF[sl], 2048), in_=src),
                  winsem[sl], writes=[("win", sl)])

        for i in range(3):
            load_item(i)
        norm([B0, B1, BE], 16)

        kT = phb(KT_OFF, 8 * 1040)
        Vt = phb(V_OFF, 9 * 1024)
        qT = phb(QT_OFF, 8 * 1024)
        pooled = phb(MISC_OFF, 8192)
        smf = phf(SM_OFF + 1024, 704)
        cqT = phb(SM_OFF, 1024)
        L_own = smf[:, 0:72]
        L_oth = smf[:, 72:136]
        cumO = smf[:, 136:200]
        kb_oth = smf[:, 200:264]
        xtot = smf[:, 264:272]
        mtot = smf[:, 272:280]
        kb_meta = smf[:, 280:288]
        ftmp = smf[:, 288:360]
        t1 = smf[:, 360:368]
        pbuf = [phf(QT_OFF + i * 2080, 1040) for i in range(4)]

        def proj(i, blocks):
            sl = i % 3
            w = phb(WIN_OFF[sl], 2048)
            banks = {b["name"]: acq() for b in blocks}
            for k in range(DC):
                lhsT = w[:, k * 128:(k + 1) * 128]
                for b in blocks:
                    bank = banks[b["name"]]
                    mm(pst[bank][:, 0:b["n"]], lhsT, xn_ap(k, b), k == 0, k == DC - 1,
                       [("win", sl), xnk(k, b)], bank, sig=(k == DC - 1))
            return banks

        def qknorm(banks, blocks, gcolumn, dst_fn, scale):
            sqs = {}
            for b in blocks:
                n = b["n"]
                bank = banks[b["name"]]
                sl = sqrot[0] % 3
                sqrot[0] += 1
                sqa = sq[:, sl * 512:sl * 512 + n]
                sqs[b["name"]] = (sl, sqa)
                S.op("act", lambda e, sqa=sqa, pa=pst[bank][:, 0:n]: e.activation(out=sqa, in_=pa, func=AF.Square),
                     reads=[PK(bank)], writes=[("sq", sl)])
            ys = {}
            for b in blocks:
                n = b["n"]
                sl, sqa = sqs[b["name"]]
                if b["name"] == "be":
                    yb = banks["be"]
                    yap = pst[yb][:, 64:64 + n]
                else:
                    yb = acq()
                    yap = pst[yb][:, 0:n]
                ys[b["name"]] = (yb, yap)
                mm(yap, onesb, sqa, True, True, [("sq", sl), "cb"], yb, sig=True)
            for b in blocks:
                n = b["n"]
                yb, yap = ys[b["name"]]
                ra = rstd[:, b["c0"]:b["c0"] + n]
                S.op("act", lambda e, ra=ra, yap=yap: e.activation(
                    out=ra, in_=yap, func=AF.Ln, bias=corec[:, 2:3], scale=1.0 / 128.0),
                    reads=[PK(yb), "corec"], writes=[("rstd", b["name"])])
                if b["name"] != "be":
                    rel(yb)
            for b in blocks:
                n = b["n"]
                ra = rstd[:, b["c0"]:b["c0"] + n]
                if scale is None:
                    S.op("act", lambda e, ra=ra: e.activation(out=ra, in_=ra, func=AF.Exp, scale=-0.5),
                         reads=[("rstd", b["name"])], writes=[("rstd", b["name"])])
                else:
                    S.op("act", lambda e, ra=ra: e.activation(
                        out=ra, in_=ra, func=AF.Exp, bias=corec[:, 3:4], scale=-0.5),
                        reads=[("rstd", b["name"]), "corec"], writes=[("rstd", b["name"])])
            for b in blocks:
                n = b["n"]
                bank = banks[b["name"]]
                ra = rstd[:, b["c0"]:b["c0"] + n]
                dst, dkey = dst_fn(b)
                S.op("dve", lambda e, dst=dst, pa=pst[bank][:, 0:n], g=qkg[:, gcolumn:gcolumn + 1], ra=ra:
                     e.scalar_tensor_tensor(out=dst, in0=pa, scalar=g, in1=ra, op0=ALU.mult, op1=ALU.mult),
                     reads=[PK(bank), ("rstd", b["name"]), "qkg"], writes=[dkey])
                rel(bank)

        ii = [0]
        pend = []

        def flush():
            while pend:
                pend.pop(0)()

        def run_chunk(blocks, post):
            banks = proj(ii[0], blocks)
            if ii[0] + 3 < len(items):
                load_item(ii[0] + 3)
            ii[0] += 1
            flush()
            pend.append(lambda: post(banks))

        def p_post(m, banks):
            g = m // 2
            w = 2 << g
            p0 = pbuf[m % 2]
            pk = ("pb", m % 2)
            S.op("act", lambda e, p0=p0, pa=pst[banks["b0"]][:, 0:512]: e.activation(
                out=p0[:, 16:528], in_=pa, func=AF.Copy), reads=[PK(banks["b0"])], writes=[pk])
            rel(banks["b0"])
            S.op("act", lambda e, p0=p0, pa=pst[banks["b1"]][:, 0:512]: e.activation(
                out=p0[:, 528:1040], in_=pa, func=AF.Copy), reads=[PK(banks["b1"])], writes=[pk])
            rel(banks["b1"])
            S.op("act", lambda e, p0=p0, pa=pst[banks["be"]][:, 16:32]: e.activation(
                out=p0[:, 0:16], in_=pa, func=AF.Copy), reads=[PK(banks["be"])], writes=[pk])
            rel(banks["be"])
            cur, curk = p0, pk
            sh = 1
            lvl = 0
            lo = 0
            while sh < w:
                nxt = pbuf[2 + lvl % 2]
                nk = ("pb", 2 + lvl % 2)
                lo2 = lo + sh
                S.op("pool", lambda e, nxt=nxt, cur=cur, lo2=lo2, sh=sh: e.tensor_tensor(
                    out=nxt[:, lo2:1040], in0=cur[:, lo2:1040], in1=cur[:, lo2 - sh:1040 - sh], op=ALU.add),
                    reads=[curk], writes=[nk])
                cur, curk = nxt, nk
                lo = lo2
                sh *= 2
                lvl += 1
            S.op("dve", lambda e, cur=cur, p0=p0, m=m, w=w: e.scalar_tensor_tensor(
                out=pooled[:, m * 1024:(m + 1) * 1024], in0=cur[:, 16:1040], scalar=1.0 / w, in1=p0[:, 16:1040],
                op0=ALU.mult, op1=ALU.subtract), reads=[curk, pk], writes=[("pooled", m)])

        for m in range(8):
            run_chunk([B0, B1, BE], lambda banks, m=m: p_post(m, banks))

        BEm = dict(BE)
        BEm["n"] = 16

        def k_post(h, banks):
            def kdst(b):
                if b["name"] == "be":
                    return kT[:, h * 1040 + 1024:h * 1040 + 1040], ("kTm", h)
                return kT[:, h * 1040 + b["c0"]:h * 1040 + b["c0"] + 512], ("kT", h, b["name"])

            qknorm(banks, [B0, B1, BEm], 1, kdst, None)

        for h in range(8):
            run_chunk([B0, B1, BE], lambda banks, h=h: k_post(h, banks))

        ptoks = S.retire([("pb", i) for i in range(4)])
        S.inherit([("vts", 0), ("vts", 1)], ptoks)
        vts = [phb(QT_OFF + i * 1056, 1056) for i in range(2)]

        def v_post(h, banks):
            st = vts[h % 2]
            sk = ("vts", h % 2)
            for b in (B0, B1, BE):
                bank = banks[b["name"]]
                S.op("act", lambda e, st=st, b=b, pa=pst[bank][:, 0:b["n"]]: e.activation(
                    out=st[:, b["c0"]:b["c0"] + b["n"]], in_=pa, func=AF.Copy),
                    reads=[PK(bank)], writes=[sk])
                rel(bank)
            tb = acq()
            tv = pst[tb][:, :].bitcast(BF16)
            for i in range(8):
                S.op("pe", lambda e, tv=tv, st=st, i=i: e.transpose(
                    tv[:, i * 128:(i + 1) * 128], st[:, i * 128:(i + 1) * 128], identb),
                    reads=[sk, "cb"], writes=[PK(tb)], sig=(i == 7))
            S.op("dve", lambda e, tv=tv, h=h: e.tensor_copy(
                out=v3(Vt[:, 0:8192], 8)[:, :, h * 128:(h + 1) * 128], in_=v3(tv, 8)),
                reads=[PK(tb)], writes=[("V", h)])
            rel(tb)
            tb2 = acq()
            tv2 = pst[tb2][:, :].bitcast(BF16)
            S.op("pe", lambda e, tv2=tv2, st=st: e.transpose(tv2[0:32, 0:128], st[:, 1024:1056], identb),
                 reads=[sk, "cb"], writes=[PK(tb2)], sig=True)
            S.op("dve", lambda e, tv2=tv2, h=h: e.tensor_copy(
                out=Vt[0:16, 8192 + h * 128:8192 + (h + 1) * 128], in_=tv2[0:16, 0:128]),
                reads=[PK(tb2)], writes=[("Vm", h)])
            rel(tb2)

        RG = [[0, 1], [2, 3], [4, 5], [6, 7]]
        for h in range(8):
            run_chunk([B0, B1, BE], lambda banks, h=h: v_post(h, banks))
            if h == 0:
                S.dma("sp", lambda e: e.dma_start(
                    out=kb_d.ap().rearrange("(h p) t -> p h t", p=128), in_=v3(kT, 8)[:, :, 0:1024]),
                    newsem(), reads=[("kT", hh, bn) for hh in range(8) for bn in ("b0", "b1")], writes=["kb_d"])
            if h == 3:
                S.dma("pool", lambda e: e.collective_compute(
                    "AllGather", ALU.bypass, replica_groups=RG, ins=[kb_d.ap().opt()], outs=[kg_d.ap().opt()]),
                    newsem(), reads=["kb_d"], writes=["kg_d"], inc=1)
        flush()

        fb = acq()
        for i in range(9):
            rows = 128 if i < 8 else NE
            for k in range(DC):
                blkname = "b0" if i < 4 else ("b1" if i < 8 else "be")
                lhsT = xn[:, k * TX + i * 128:k * TX + i * 128 + rows]
                S.op("pe", lambda e, lhsT=lhsT, rows=rows, i=i, k=k: e.matmul(
                    pst[fb][0:rows, i * 8:(i + 1) * 8], lhsT, wf[:, k * 8:(k + 1) * 8],
                    start=(k == 0), stop=(k == DC - 1)),
                    reads=[("xn", k, blkname), "wf"], writes=[PK(fb)], sig=(k == DC - 1 and i == 8))
        for i in range(9):
            S.op("dve", lambda e, i=i: e.tensor_tensor(
                out=ftmp[:, i * 8:(i + 1) * 8], in0=pst[fb][:, i * 8:(i + 1) * 8], in1=bfr[:, :], op=ALU.add),
                reads=[PK(fb), "bfr"], writes=["ftmp"])
        rel(fb)
        S.op("act", lambda e: e.activation(out=ftmp, in_=ftmp, func=AF.Exp, scale=-1.0),
             reads=["ftmp"], writes=["ftmp"])
        S.op("act", lambda e: e.activation(out=L_own, in_=ftmp, func=AF.Ln, bias=onesf[:, 0:1], scale=1.0),
             reads=["ftmp", "onesf"], writes=["L_own"])

        S.dma("sp", lambda e: e.dma_start(
            out=vb_d.ap().rearrange("(i p) c -> p i c", p=128), in_=v3(Vt[:, 0:8192], 8)),
            newsem(), reads=[("V", h) for h in range(8)], writes=["vb_d"])
        S.dma("sp", lambda e: e.dma_start(
            out=lb_d.ap().rearrange("(i p) c -> p i c", p=128), in_=v3(L_own[:, 0:64], 8)),
            newsem(), reads=["L_own"], writes=["lb_d"])

        def gather_vl():
            for src, dst, ks, kd in ((vb_d, vg_d, "vb_d", "vg_d"), (lb_d, lg_d, "lb_d", "lg_d")):
                S.dma("pool", lambda e, src=src, dst=dst: e.collective_compute(
                    "AllGather", ALU.bypass, replica_groups=RG, ins=[src.ap().opt()], outs=[dst.ap().opt()]),
                    newsem(), reads=[ks], writes=[kd], inc=1)
            S.dma("sp", lambda e: e.dma_start(
                out=v3(L_oth, 8), in_=lg_d.ap()[0:1024, :].rearrange("(i p) c -> p i c", p=128)),
                newsem(), reads=["lg_d"], writes=["L_oth"])

        S.inherit([("qT", h, bn) for h in range(8) for bn in ("b0", "b1")],
                  S.retire([("vts", 0), ("vts", 1)]) + ptoks)
        def q_post(h, banks):
            def qdst(b):
                return qT[:, h * 1024 + b["c0"]:h * 1024 + b["c0"] + 512], ("qT", h, b["name"])

            qknorm(banks, [B0, B1], 0, qdst, ATT_SCALE)

        for h in range(8):
            run_chunk([B0, B1], lambda banks, h=h: q_post(h, banks))
            if h == 1:
                gather_vl()
        flush()

        for g in range(4):
            for dc in range(2):
                m = g * 2 + dc
                for b in (B0, B1):
                    bank = acq()
                    for kc in range(2):
                        lhsT = poolw[:, (g * 2 + kc) * 256 + dc * 128:(g * 2 + kc) * 256 + (dc + 1) * 128]
                        rhs = pooled[:, (g * 2 + kc) * 1024 + b["c0"]:(g * 2 + kc) * 1024 + b["c0"] + 512]
                        mm(pst[bank][:, 0:512], lhsT, rhs, kc == 0, kc == 1,
                           ["poolw", ("pooled", g * 2 + kc)], bank, sig=(kc == 1))
                    S.op("act", lambda e, m=m, b=b, pa=pst[bank][:, 0:512]: e.activation(
                        out=xn_ap(m, b), in_=pa, func=AF.Copy, scale=pscale[:, m:m + 1]),
                        reads=[PK(bank), "pscale"], writes=[xnk(m, b)])
                    rel(bank)

        zb = acq()
        Z = pst[zb]
        nmm = [0]

        def zmm(outp, lhsT, rhs, start, stop, reads, last=False):
            S.op("pe", lambda e: e.matmul(outp, lhsT, rhs, start=start, stop=stop),
                 reads=reads, writes=[PK(zb)], sig=last)

        for i in range(8):
            for i2 in range(i):
                zmm(Z[:, i * 8:(i + 1) * 8], onesf[:, :], L_own[:, i2 * 8:(i2 + 1) * 8], i2 == 0, False,
                    ["onesf", "L_own"])
            zmm(Z[:, i * 8:(i + 1) * 8], tri[:, :], L_own[:, i * 8:(i + 1) * 8], i == 0, True, ["tri", "L_own"])
        for i in range(8):
            for i2 in range(i):
                zmm(Z[:, 64 + i * 8:64 + (i + 1) * 8], onesf[:, :], L_oth[:, i2 * 8:(i2 + 1) * 8], i2 == 0, False,
                    ["onesf", "L_oth"])
            zmm(Z[:, 64 + i * 8:64 + (i + 1) * 8], tri[:, :], L_oth[:, i * 8:(i + 1) * 8], i == 0, True,
                ["tri", "L_oth"])
        for i in range(8):
            zmm(Z[:, 128:136], onesf[:, :], L_oth[:, i * 8:(i + 1) * 8], i == 0, i == 7, ["onesf", "L_oth"])
        zmm(Z[0:16, 136:144], tri[0:16, 0:16], L_own[0:16, 64:72], True, True, ["tri", "L_own"])
        zmm(Z[:, 144:152], onesf[0:16, :], L_own[0:16, 64:72], True, True, ["onesf", "L_own"], last=True)
        S.op("dve", lambda e: e.tensor_copy(out=cumO, in_=Z[:, 0:64]), reads=[PK(zb)], writes=["cumO"])
        S.op("dve", lambda e: e.tensor_copy(out=xtot, in_=Z[:, 128:136]), reads=[PK(zb)], writes=["xtot"])
        S.op("dve", lambda e: e.tensor_copy(out=mtot, in_=Z[:, 144:152]), reads=[PK(zb)], writes=["mtot"])
        for i in range(8):
            S.op("dve", lambda e, i=i: e.scalar_tensor_tensor(
                out=kb_oth[:, i * 8:(i + 1) * 8], in0=Z[:, 64 + i * 8:64 + (i + 1) * 8], scalar=corec[:, 0:1],
                in1=xtot, op0=ALU.add, op1=ALU.subtract),
                reads=[PK(zb), "xtot", "corec"], writes=["kb_oth"])
        S.op("dve", lambda e: e.scalar_tensor_tensor(
            out=t1, in0=xtot, scalar=corec[:, 1:2], in1=mtot, op0=ALU.mult, op1=ALU.add),
            reads=["xtot", "mtot", "corec"], writes=["t1"])
        S.op("dve", lambda e: e.tensor_tensor(
            out=kb_meta[0:16, :], in0=Z[0:16, 136:144], in1=t1[0:16, :], op=ALU.subtract),
            reads=[PK(zb), "t1"], writes=["kb_meta"])
        rel(zb)
        S.op("pool", lambda e: e.memset(cqT, 0.0), writes=[("cqT", 0), ("cqT", 1)])
        for half in range(2):
            cbk = acq()
            for i in range(4):
                it = half * 4 + i
                S.op("pe", lambda e, cbk=cbk, i=i, it=it: e.transpose(
                    pst[cbk][0:8, i * 128:(i + 1) * 128], cumO[:, it * 8:(it + 1) * 8], identf[:, :]),
                    reads=["cumO", "identf"], writes=[PK(cbk)], sig=(i == 3))
            S.op("dve", lambda e, cbk=cbk, half=half: e.tensor_scalar(
                out=cqT[0:8, half * 512:(half + 1) * 512], in0=pst[cbk][0:8, 0:512], scalar1=-1.0, scalar2=None,
                op0=ALU.mult), reads=[PK(cbk)], writes=[("cqT", half)])
            rel(cbk)

        atoks = S.retire([("pooled", m) for m in range(8)])
        osl = [phb(MISC_OFF + i * 2048, 2048) for i in range(2)]
        PTs = [phb(MISC_OFF + 4096 + i * 512, 512) for i in range(3)]
        rinvs = [phf(MISC_OFF + 4096 + 1536 + i * 1024, 512) for i in range(2)]
        S.inherit([("osl", 0), ("osl", 1)] + [("PT", i) for i in range(3)] + [("rinv", i) for i in range(2)], atoks)
        osem = [newsem() for _ in range(4)]

        def load_other(h):
            sl = h % 2
            S.dma("sp", lambda e, sl=sl, h=h: e.dma_start(
                out=osl[sl][:, 0:1024], in_=kg_d.ap()[h * 128:(h + 1) * 128, :]),
                osem[sl * 2], reads=["kg_d"], writes=[("osl", sl)])
            S.dma("sp", lambda e, sl=sl, h=h: e.dma_start(
                out=v3(osl[sl][:, 1024:2048], 8),
                in_=vg_d.ap()[0:1024, h * 128:(h + 1) * 128].rearrange("(i p) c -> p i c", p=128)),
                osem[sl * 2 + 1], reads=["vg_d"], writes=[("oslv", sl)])

        load_other(0)
        ptrot = [0]
        rirot = [0]
        work = []
        for h in range(8):
            for s in range(2):
                tiles = [("m", 0)] + [("o", i) for i in range(4 * (s + 1))] + [("x", i) for i in range(8)]
                for ti, tl in enumerate(tiles):
                    work.append((h, s, tl, ti == 0, ti == len(tiles) - 1))
        LA = 2
        orb = {}

        def qk_issue(w):
            h, s, (kind, i), first, last = w
            sl = h % 2
            qb = B0 if s == 0 else B1
            q_ap = qT[:, h * 1024 + s * 512:h * 1024 + (s + 1) * 512]
            qk = ("qT", h, qb["name"])
            sb_ = acq()
            if kind == "m":
                kt_ = 16
                lhsT = kT[:, h * 1040 + 1024:h * 1040 + 1040]
                kr = [("kTm", h)]
            elif kind == "o":
                kt_ = 128
                lhsT = kT[:, h * 1040 + i * 128:h * 1040 + (i + 1) * 128]
                kr = [("kT", h, "b0" if i < 4 else "b1")]
            else:
                kt_ = 128
                lhsT = osl[sl][:, i * 128:(i + 1) * 128]
                kr = [("osl", sl)]
            dg = (kind == "o" and i // 4 == s)
            mm(pst[sb_][0:kt_, :], lhsT, q_ap, True, False, kr + [qk], sb_)
            mm(pst[sb_][0:kt_, :], esel[:, h * 128:h * 128 + kt_], cqT[:, s * 512:(s + 1) * 512],
               False, not dg, ["esel", ("cqT", s)], sb_, sig=not dg)
            if dg:
                mm(pst[sb_][0:kt_, :], identb, diag(i % 4), False, True, ["cb"], sb_, sig=True)
            return sb_, kt_

        def pv_issue(w, sb_, kt_):
            h, s, (kind, i), first, last = w
            sl = h % 2
            qb = B0 if s == 0 else B1
            if first:
                orb[(h, s)] = (acq(), acq())
            ob, rb = orb[(h, s)]
            if kind == "m":
                bias = kb_meta[0:16, h:h + 1]
                bk = "kb_meta"
                vl = Vt[0:16, 8192 + h * 128:8192 + (h + 1) * 128]
                vk = [("Vm", h)]
            elif kind == "o":
                bias = cumO[:, i * 8 + h:i * 8 + h + 1]
                bk = "cumO"
                vl = Vt[:, i * 1024 + h * 128:i * 1024 + (h + 1) * 128]
                vk = [("V", h)]
            else:
                bias = kb_oth[:, i * 8 + h:i * 8 + h + 1]
                bk = "kb_oth"
                vl = osl[sl][:, 1024 + i * 128:1024 + (i + 1) * 128]
                vk = [("oslv", sl)]
            pi = ptrot[0] % 3
            ptrot[0] += 1
            pt = PTs[pi][0:kt_, :]
            S.op("act", lambda e, pt=pt, pa=pst[sb_][0:kt_, :], bias=bias: e.activation(
                out=pt, in_=pa, func=AF.Exp, bias=bias, scale=1.0),
                reads=[PK(sb_), bk], writes=[("PT", pi)])
            rel(sb_)
            mm(pst[ob][:, :], vl, pt, first, last, vk + [("PT", pi)], ob, sig=False)
            mm(pst[rb][:, :], onesb[0:kt_, :], pt, first, last, ["cb", ("PT", pi)], rb, sig=last)
            if last:
                ri = rirot[0] % 2
                rirot[0] += 1
                S.op("dve", lambda e, ri=ri, rb=rb: e.reciprocal(out=rinvs[ri], in_=pst[rb][:, :]),
                     reads=[PK(rb)], writes=[("rinv", ri)])
                rel(rb)
                S.op("dve", lambda e, ri=ri, h=h, qb=qb, ob=ob: e.tensor_tensor(
                    out=xn_ap(8 + h, qb), in0=pst[ob][:, :], in1=rinvs[ri], op=ALU.mult),
                    reads=[PK(ob), ("rinv", ri)], writes=[xnk(8 + h, qb)])
                rel(ob)

        issued = []
        nq = 0
        loaded = {0}
        for wi, w in enumerate(work):
            h = w[0]
            if w[3] and w[1] == 0 and h + 1 < 8 and (h + 1) not in loaded:
                load_other(h + 1)
                loaded.add(h + 1)
            while nq < min(wi + 1 + LA, len(work)):
                issued.append(qk_issue(work[nq]))
                nq += 1
            sb_, kt_ = issued[wi]
            pv_issue(w, sb_, kt_)

        def o_post(c, banks):
            for b in (B0, B1):
                bank = banks[b["name"]]
                hsrc = b["src"](c)
                S.op("dve", lambda e, hsrc=hsrc, pa=pst[bank][:, 0:512]: e.tensor_tensor(
                    out=hsrc, in0=pa, in1=hsrc, op=ALU.add),
                    reads=[PK(bank)] + b["hk"](c), writes=[("hc", c)])
                rel(bank)

        for c in range(16):
            run_chunk([B0, B1], lambda banks, c=c: o_post(c, banks))
        flush()

    if STAGE >= 2:
        mixer()
        S.barrier()
    if STAGE >= 3:
        ffn(1, [B0, B1], 32, final_store=True)
    else:
        for c4 in range(4):
            out_toks.append(S.dma(
                "sp", lambda e, c4=c4: e.dma_start(
                    out=out_d[c4 * 512:(c4 + 1) * 512, :].rearrange("(c p) t -> p c t", p=128),
                    in_=v3(hT[:, c4 * 4 * T:(c4 + 1) * 4 * T], 4)),
                newsem(), reads=[("hc", cc) for cc in range(c4 * 4, c4 * 4 + 4)] + [("h", 0), ("h", 1)]))
    S.barrier()
    S.q["sp"].append(([(t.sem, t.val) for t in out_toks], None, 3, None))

    with nc.Block() as block:
        def runner(name):
            def f(engine):
                for waits, fn, kind, info in S.q[name]:
                    for sem, val in waits:
                        engine.wait_ge(sem, val)
                    if fn is None:
                        continue
                    ins = fn(engine)
                    if kind == 1:
                        ins.then_inc(esem[name], 1)
                    elif kind == 2:
                        if info[1] == 16:
                            ins.then_inc(info[0], 16)
                        else:
                            ins.then_inc(info[0])
            return f

        block.tensor(runner("pe"))
        block.scalar(runner("act"))
        block.vector(runner("dve"))
        block.gpsimd(runner("pool"))
        block.sync(runner("sp"))
    return nc


def _prep_shared(inp):
    f = np.float32
    sh = {}
    g = np.zeros((128, 48), f)
    for i, nm in enumerate(("ffn1_norm", "mix_norm", "ffn2_norm")):
        g[:, i * 16:(i + 1) * 16] = np.asarray(inp[nm], f)[0].reshape(16, 128).T
    sh["gains"] = g
    sh["qkg"] = np.stack([np.asarray(inp["q_norm"], f)[0], np.asarray(inp["k_norm"], f)[0]], axis=1).copy()
    sh["pscale"] = np.ascontiguousarray(np.asarray(inp["pool_scale"], f)[0].reshape(8, 128).T)
    sh["bfr"] = np.ascontiguousarray(np.broadcast_to(np.asarray(inp["b_forget"], f)[0][None, :], (128, 8)))
    sh["identf"] = np.eye(128, dtype=f)
    sh["onesf"] = np.ones((128, 128), f)
    sh["tri"] = np.triu(np.ones((128, 128), f))
    cbf = np.zeros((128, 2304), f)
    cbf[:, 0:128] = np.eye(128, dtype=f)
    cbf[:, 128:256] = 1.0
    kk = np.arange(128)[:, None]
    qq = np.arange(512)[None, :]
    for i in range(4):
        cbf[:, 256 + i * 512:256 + (i + 1) * 512] = np.where(qq >= kk + 128 * i, 0.0, NEG)
    sh["cbf"] = cbf
    es = np.zeros((128, 1024), f)
    for h in range(8):
        es[h, h * 128:(h + 1) * 128] = 1.0
    sh["esel"] = es
    for i, pre in enumerate(("ffn1", "ffn2")):
        wg = np.asarray(inp[pre + "_w_gate"], f)[0].reshape(16, 128, FC, 128).transpose(2, 1, 0, 3)
        wu = np.asarray(inp[pre + "_w_up"], f)[0].reshape(16, 128, FC, 128).transpose(2, 1, 0, 3)
        sh[f"wgu{i + 1}"] = np.ascontiguousarray(np.stack([wg, wu], axis=2)).reshape(FC, 128, 4096)
        wd = np.asarray(inp[pre + "_w_down"], f)[0].reshape(NG, GJ, 128, 8, 256).transpose(0, 3, 2, 1, 4)
        sh[f"wd{i + 1}"] = np.ascontiguousarray(wd).reshape(NG * 8, 128, GJ * 256)
    w_in = np.asarray(inp["w_in"], f)[0]
    sh["win"] = np.ascontiguousarray(
        w_in[:, 0:4096].reshape(16, 128, 32, 128).transpose(2, 1, 0, 3)).reshape(32, 128, 2048)
    sh["wf"] = np.ascontiguousarray(w_in[:, 4096:4104].reshape(16, 128, 8).transpose(1, 0, 2)).reshape(128, 128)
    sh["wout"] = np.ascontiguousarray(
        np.asarray(inp["w_out"], f)[0].reshape(16, 128, 16, 128).transpose(2, 1, 0, 3)).reshape(16, 128, 2048)
    pw = np.asarray(inp["pool_w"], f)[0].reshape(4, 2, 128, 256).transpose(2, 0, 1, 3)
    sh["poolw"] = np.ascontiguousarray(pw).reshape(128, 2048)
    return sh


_NC_CACHE = {}


def kernel(**inputs):
    x = np.asarray(inputs["x"], np.float32)
    meta = np.asarray(inputs["meta_tokens"], np.float32)
    sh = _prep_shared(inputs)
    in_maps = []
    for core in range(8):
        b, r = core // 2, core % 2
        m = dict(sh)
        m["xT"] = np.ascontiguousarray(x[b, r * T:(r + 1) * T, :].T)
        hist = meta if r == 0 else x[b, T - 16:T, :]
        m["xeT"] = np.ascontiguousarray(np.concatenate([meta, hist], axis=0).T)
        cc = np.zeros((128, 4), np.float32)
        cc[:, 0] = NEG if r == 0 else 0.0
        cc[:, 1] = 0.0 if r == 0 else 1.0
        cc[:, 2] = EPS
        cc[:, 3] = np.log(ATT_SCALE)
        m["corec"] = cc
        in_maps.append(m)
    if "nc" not in _NC_CACHE:
        _NC_CACHE["nc"] = build_program()
    nc = _NC_CACHE["nc"]
    res = run_bass_kernel_spmd(nc, in_maps, core_ids=list(range(8)))
    out = np.empty((4, 2048, D), np.float32)
    for core in range(8):
        b, r = core // 2, core % 2
        out[b, r * T:(r + 1) * T, :] = res.results[core]["outT"].T
    return out
```

```python
import os
import numpy as np
import concourse.bass as bass
import concourse.mybir as mybir
from concourse.bass_utils import run_bass_kernel_spmd

F32 = mybir.dt.float32
BF16 = mybir.dt.bfloat16
ALU = mybir.AluOpType
AF = mybir.ActivationFunctionType

D = 2048
DC = 16
FF = 5632
FC = 44
NG = 4
GJ = 11
T = 1024
NE = 32
TX = T + NE
NH = 8
EPS = 1e-6
NEG = -30000.0
ATT_SCALE = float(1.0 / np.sqrt(128.0))

ENG = ("pe", "act", "dve", "pool", "sp")


class Tok:
    __slots__ = ("key", "sem", "val")

    def __init__(self, key, sem, val):
        self.key, self.sem, self.val = key, sem, val


class Sched:
    def __init__(self, esem):
        self.q = {e: [] for e in ENG}
        self.esem = esem
        self.ecnt = {e: 0 for e in ENG}
        self.waited = {e: {} for e in ENG}
        self.lastw = {}
        self.lastr = {}
        self.dcnt = {}
        self.dsem = {}
        self.bar = []

    def _deps(self, eng, reads, writes, extra):
        need = {}

        def add(t):
            if t is None:
                return
            if t.key == eng and eng == "pe":
                return
            o = need.get(t.key)
            if o is None or o.val < t.val:
                need[t.key] = t

        for k in reads:
            add(self.lastw.get(k))
        for k in writes:
            add(self.lastw.get(k))
            for t in self.lastr.get(k, {}).values():
                add(t)
        for t in extra:
            add(t)
        for t in self.bar:
            add(t)
        waits = []
        for key, t in need.items():
            if self.waited[eng].get(key, 0) < t.val:
                self.waited[eng][key] = t.val
                waits.append((t.sem, t.val))
        return waits

    def _record(self, tok, reads, writes):
        for k in reads:
            d = self.lastr.setdefault(k, {})
            o = d.get(tok.key)
            if o is None or o.val < tok.val:
                d[tok.key] = tok
        for k in writes:
            self.lastw[k] = tok
            self.lastr[k] = {}

    def op(self, eng, fn, reads=(), writes=(), sig=True, extra=()):
        sig = True
        waits = self._deps(eng, reads, writes, extra)
        if sig:
            self.ecnt[eng] += 1
            tok = Tok(eng, self.esem[eng], self.ecnt[eng])
        else:
            tok = Tok(eng, self.esem[eng], self.ecnt[eng] + 1)
        self.q[eng].append((waits, fn, 1 if sig else 0, None))
        self._record(tok, reads, writes)
        return tok

    def dma(self, eng, fn, sem, reads=(), writes=(), extra=(), inc=16):
        waits = self._deps(eng, reads, writes, extra)
        k = id(sem)
        self.dcnt[k] = self.dcnt.get(k, 0) + inc
        self.dsem[k] = sem
        tok = Tok(("d", k), sem, self.dcnt[k])
        self.q[eng].append((waits, fn, 2, (sem, inc)))
        self._record(tok, reads, writes)
        return tok

    def barrier(self):
        b = []
        for e in ENG:
            if self.ecnt[e] > 0:
                b.append(Tok(e, self.esem[e], self.ecnt[e]))
        for k, c in self.dcnt.items():
            b.append(Tok(("d", k), self.dsem[k], c))
        self.bar = b

    def retire(self, keys):
        toks = []
        for k in keys:
            t = self.lastw.get(k)
            if t is not None:
                toks.append(t)
            toks.extend(self.lastr.get(k, {}).values())
        return toks

    def inherit(self, keys, toks):
        for k in keys:
            d = self.lastr.setdefault(k, {})
            for t in toks:
                o = d.get(t.key)
                if o is None or o.val < t.val:
                    d[t.key] = t

    def replay(self, eng, engine):
        for waits, fn, kind, info in self.q[eng]:
            for sem, val in waits:
                engine.wait_ge(sem, val)
            ins = fn(engine)
            if kind == 1:
                ins.then_inc(self.esem[eng], 1)
            elif kind == 2:
                if info[1] == 16:
                    ins.then_inc(info[0], 16)
                else:
                    ins.then_inc(info[0])


def v3(ap, a):
    return ap.rearrange("p (a b) -> p a b", a=a)


STAGE = int(os.environ.get("MK_STAGE", "9"))


def build_program():
    nc = bass.Bass("TRN2", target_bir_lowering=False)

    def din(name, shape, dt=F32):
        return nc.dram_tensor(name, list(shape), dt, kind="ExternalInput").ap()

    xT_d = din("xT", [D, T])
    xe_d = din("xeT", [D, NE])
    gains_d = din("gains", [128, 48])
    qkg_d = din("qkg", [128, 2])
    pscale_d = din("pscale", [128, 8])
    bfr_d = din("bfr", [128, 8])
    core_d = din("corec", [128, 4])
    identf_d = din("identf", [128, 128])
    onesf_d = din("onesf", [128, 128])
    tri_d = din("tri", [128, 128])
    cb_d = din("cbf", [128, 2304])
    esel_d = din("esel", [128, 1024])
    wgu_d = [din("wgu1", [FC, 128, 4096]), din("wgu2", [FC, 128, 4096])]
    wd_d = [din("wd1", [NG * 8, 128, GJ * 256]), din("wd2", [NG * 8, 128, GJ * 256])]
    win_d = din("win", [32, 128, 2048])
    wf_d = din("wf", [128, 128])
    wout_d = din("wout", [16, 128, 2048])
    poolw_d = din("poolw", [128, 2048])
    out_d = nc.dram_tensor("outT", [D, T], F32, kind="ExternalOutput").ap()

    kb_d = nc.dram_tensor("kb", [1024, T], BF16)
    vb_d = nc.dram_tensor("vb", [1024, 1024], BF16)
    lb_d = nc.dram_tensor("lb", [1024, 8], F32)
    kg_d = nc.dram_tensor("kg", [2048, T], BF16)
    vg_d = nc.dram_tensor("vg", [2048, 1024], BF16)
    lg_d = nc.dram_tensor("lg", [2048, 8], F32)

    def sb(name, cols, dt):
        return nc.alloc_sbuf_tensor("sb_" + name, [128, cols], dt)

    hT = sb("hT", DC * T, F32)
    hE = sb("hE", DC * NE, F32)
    xn = sb("xn", DC * TX, BF16)
    identf = sb("identf", 128, F32)
    onesf = sb("onesf", 128, F32)
    tri = sb("tri", 128, F32)
    cb = sb("cb", 2304, BF16)
    esel = sb("esel", 1024, BF16)
    gains = sb("gains", 48, F32)
    qkg = sb("qkg", 2, F32)
    pscale = sb("pscale", 8, F32)
    bfr = sb("bfr", 8, F32)
    corec = sb("corec", 4, F32)
    poolw = sb("poolw", 2048, BF16)
    wf = sb("wf", 128, BF16)
    rstd = sb("rstd", TX, F32)
    sq = sb("sq", 3 * 512, BF16)
    PH = 43008
    ph = sb("ph", PH, BF16)

    identb = cb[:, 0:128]
    onesb = cb[:, 128:256]

    def diag(i):
        return cb[:, 256 + i * 512:256 + (i + 1) * 512]

    o = 0
    WGU_OFF = [o + i * 4096 for i in range(3)]; o += 3 * 4096
    WD_OFF = [o + i * 2816 for i in range(3)]; o += 3 * 2816
    SILU_OFF = [o + i * 1024 for i in range(3)]; o += 3 * 1024
    ACT_OFF = o; o += GJ * TX
    assert o <= PH, o
    o = 0
    WIN_OFF = [o + i * 2048 for i in range(3)]; o += 3 * 2048
    MISC_OFF = o; o += 8192
    QT_OFF = o; o += 8320
    KT_OFF = o; o += 8 * 1040
    V_OFF = o; o += 9 * 1024
    SM_OFF = o; o += 2560
    assert o <= PH, o

    def phb(off, n):
        return ph[:, off:off + n]

    def phf(off, n):
        return ph[:, off:off + 2 * n].bitcast(F32)

    pst = [nc.alloc_psum_tensor(f"ps{i}", [128, 512], F32) for i in range(8)]

    esem = {e: nc.alloc_semaphore(name=f"s_{e}") for e in ENG}
    S = Sched(esem)
    nsem = [0]

    def newsem():
        nsem[0] += 1
        return nc.alloc_semaphore(name=f"d{nsem[0]}")

    free_banks = list(range(8))

    def acq():
        return free_banks.pop(0)

    def rel(b):
        free_banks.append(b)

    def PK(b):
        return ("ps", b)

    def dma_in(eng, dst, src, key, sem=None):
        sem = sem or newsem()
        return S.dma(eng, lambda e, dst=dst, src=src: e.dma_start(out=dst, in_=src), sem, writes=[key])

    def mm(out, lhsT, rhs, start, stop, reads, bank, sig=False):
        return S.op("pe", lambda e: e.matmul(out, lhsT, rhs, start=start, stop=stop),
                    reads=reads, writes=[PK(bank)], sig=sig)

    for bi in range(2):
        for c4 in range(4):
            dma_in("sp", v3(hT[:, c4 * 4 * T:(c4 + 1) * 4 * T], 4)[:, :, bi * 512:(bi + 1) * 512],
                   xT_d[c4 * 512:(c4 + 1) * 512, bi * 512:(bi + 1) * 512].rearrange("(c p) t -> p c t", p=128),
                   ("h", bi, c4))
    dma_in("sp", v3(hE[:, :], DC), xe_d.rearrange("(c p) t -> p c t", p=128), "hE")
    for dst, src, key in ((gains, gains_d, "gains"), (qkg, qkg_d, "qkg"), (pscale, pscale_d, "pscale"),
                          (bfr, bfr_d, "bfr"), (corec, core_d, "corec"), (identf, identf_d, "identf"),
                          (onesf, onesf_d, "onesf"), (tri, tri_d, "tri")):
        dma_in("sp", dst[:, :], src, key)
    dma_in("pool", cb[:, :], cb_d, "cb")
    dma_in("pool", esel[:, :], esel_d, "esel")
    dma_in("pool", poolw[:, :], poolw_d, "poolw")
    dma_in("pool", wf[:, :], wf_d, "wf")

    def hkeys_for(c, bi):
        return [("h", bi, c // 4), ("hc", c)]

    def own_src(c, c0, n):
        return hT[:, c * T + c0:c * T + c0 + n]

    B0 = dict(name="b0", c0=0, n=512, src=lambda c: own_src(c, 0, 512), hk=lambda c: hkeys_for(c, 0))
    B1 = dict(name="b1", c0=512, n=512, src=lambda c: own_src(c, 512, 512), hk=lambda c: hkeys_for(c, 1))
    BE = dict(name="be", c0=1024, n=NE, src=lambda c: hE[:, c * NE:(c + 1) * NE], hk=lambda c: ["hE", ("hEc", c)])

    def xn_ap(c, blk):
        return xn[:, c * TX + blk["c0"]:c * TX + blk["c0"] + blk["n"]]

    def xnk(c, blk):
        return ("xn", c, blk["name"])

    sqrot = [0]

    def norm(blocks, gcol):
        banks = {blk["name"]: acq() for blk in blocks}
        for blk in blocks:
            n = blk["n"]
            bank = banks[blk["name"]]
            for c in range(DC):
                sl = sqrot[0] % 3
                sqrot[0] += 1
                sqa = sq[:, sl * 512:sl * 512 + n]
                S.op("act", lambda e, sqa=sqa, src=blk["src"](c): e.activation(out=sqa, in_=src, func=AF.Square),
                     reads=blk["hk"](c), writes=[("sq", sl)])
                mm(pst[bank][:, 0:n], onesb, sqa, c == 0, c == DC - 1, [("sq", sl), "cb"], bank, sig=(c == DC - 1))
        for blk in blocks:
            n = blk["n"]
            bank = banks[blk["name"]]
            ra = rstd[:, blk["c0"]:blk["c0"] + n]
            S.op("act", lambda e, ra=ra, pa=pst[bank][:, 0:n]: e.activation(
                out=ra, in_=pa, func=AF.Ln, bias=corec[:, 2:3], scale=1.0 / D),
                reads=[PK(bank), "corec"], writes=[("rstd", blk["name"])])
            rel(bank)
        for blk in blocks:
            ra = rstd[:, blk["c0"]:blk["c0"] + blk["n"]]
            S.op("act", lambda e, ra=ra: e.activation(out=ra, in_=ra, func=AF.Exp, scale=-0.5),
                 reads=[("rstd", blk["name"])], writes=[("rstd", blk["name"])])
        for c in range(DC):
            for blk in blocks:
                ra = rstd[:, blk["c0"]:blk["c0"] + blk["n"]]
                S.op("dve", lambda e, o_=xn_ap(c, blk), src=blk["src"](c), g=gains[:, gcol + c:gcol + c + 1], ra=ra:
                     e.scalar_tensor_tensor(out=o_, in0=src, scalar=g, in1=ra, op0=ALU.mult, op1=ALU.mult),
                     reads=blk["hk"](c) + [("rstd", blk["name"]), "gains"], writes=[xnk(c, blk)])

    silurot = [0]

    def ffn(fi, blocks, gcol, final_store=False):
        wgu_src = wgu_d[fi]
        wd_src = wd_d[fi]
        wsem = [newsem() for _ in range(3)]
        dsem = [newsem() for _ in range(3)]

        def load_wgu(j):
            sl = j % 3
            S.dma("pool", lambda e, sl=sl, j=j: e.dma_start(out=phb(WGU_OFF[sl], 4096), in_=wgu_src[j]),
                  wsem[sl], writes=[("wgu", sl)])

        def load_wd(n):
            sl = n % 3
            S.dma("pool", lambda e, sl=sl, n=n: e.dma_start(out=phb(WD_OFF[sl], 2816), in_=wd_src[n]),
                  dsem[sl], writes=[("wd", sl)])

        for j in range(3):
            load_wgu(j)
        for n in range(3):
            load_wd(n)
        norm(blocks, gcol)
        own = [b for b in blocks if b["name"] != "be"]
        has_e = len(own) != len(blocks)
        for G in range(NG):
            for jj in range(GJ):
                j = G * GJ + jj
                sl = j % 3
                w = phb(WGU_OFF[sl], 4096)
                gb = {b["name"]: acq() for b in own}
                eb = acq() if has_e else None
                ub = {b["name"]: acq() for b in own}
                for half, banks, ecol in ((0, gb, 0), (1, ub, 64)):
                    for k in range(DC):
                        lhsT = w[:, half * 2048 + k * 128:half * 2048 + (k + 1) * 128]
                        for b in blocks:
                            if b["name"] == "be":
                                bank = eb
                                outp = pst[eb][:, ecol:ecol + NE]
                            else:
                                bank = banks[b["name"]]
                                outp = pst[bank][:, 0:b["n"]]
                            mm(outp, lhsT, xn_ap(k, b), k == 0, k == DC - 1,
                               [("wgu", sl), xnk(k, b)], bank, sig=(k == DC - 1))
                for b in blocks:
                    n = b["n"]
                    ss = silurot[0] % 3
                    silurot[0] += 1
                    tmp = phf(SILU_OFF[ss], 512)[:, 0:n]
                    if b["name"] == "be":
                        gsrc, usrc, gk, uk = pst[eb][:, 0:NE], pst[eb][:, 64:64 + NE], eb, eb
                    else:
                        gk, uk = gb[b["name"]], ub[b["name"]]
                        gsrc, usrc = pst[gk][:, 0:n], pst[uk][:, 0:n]
                    S.op("act", lambda e, tmp=tmp, gsrc=gsrc: e.activation(out=tmp, in_=gsrc, func=AF.Silu),
                         reads=[PK(gk)], writes=[("silu", ss)])
                    dst = phb(ACT_OFF + jj * TX + b["c0"], n)
                    S.op("dve", lambda e, dst=dst, tmp=tmp, usrc=usrc: e.tensor_tensor(
                        out=dst, in0=tmp, in1=usrc, op=ALU.mult),
                        reads=[("silu", ss), PK(uk)], writes=[("act", jj, b["name"])])
                for b in own:
                    rel(gb[b["name"]])
                if has_e:
                    rel(eb)
                for b in own:
                    rel(ub[b["name"]])
                if j + 3 < FC:
                    load_wgu(j + 3)
            for cp in range(8):
                n_ = G * 8 + cp
                sl = n_ % 3
                w = phb(WD_OFF[sl], 2816)
                for ci in range(2):
                    c = cp * 2 + ci
                    banks = {b["name"]: acq() for b in blocks}
                    for jj in range(GJ):
                        lhsT = w[:, jj * 256 + ci * 128:jj * 256 + (ci + 1) * 128]
                        for b in blocks:
                            bank = banks[b["name"]]
                            mm(pst[bank][:, 0:b["n"]], lhsT, phb(ACT_OFF + jj * TX + b["c0"], b["n"]),
                               jj == 0, jj == GJ - 1, [("wd", sl), ("act", jj, b["name"])], bank,
                               sig=(jj == GJ - 1))
                    for b in blocks:
                        bank = banks[b["name"]]
                        hsrc = b["src"](c)
                        hk = ("hc", c) if b["name"] != "be" else ("hEc", c)
                        S.op("dve", lambda e, hsrc=hsrc, pa=pst[bank][:, 0:b["n"]]: e.scalar_tensor_tensor(
                            out=hsrc, in0=pa, scalar=0.5, in1=hsrc, op0=ALU.mult, op1=ALU.add),
                            reads=[PK(bank)] + b["hk"](c), writes=[hk])
                        rel(bank)
                    if final_store and G == NG - 1 and c % 2 == 1:
                        c2 = c // 2
                        out_toks.append(S.dma(
                            "sp", lambda e, c2=c2: e.dma_start(
                                out=out_d[c2 * 256:(c2 + 1) * 256, :].rearrange("(c p) t -> p c t", p=128),
                                in_=v3(hT[:, c2 * 2 * T:(c2 + 1) * 2 * T], 2)),
                            newsem(), reads=[("hc", c - 1), ("hc", c), ("h", 0, c // 4), ("h", 1, c // 4)]))
                if n_ + 3 < NG * 8:
                    load_wd(n_ + 3)

    out_toks = []

    if STAGE >= 1:
        ffn(0, [B0, B1, BE], 0)
        S.barrier()
    elif STAGE == 0:
        norm([B0, B1, BE], 0)

    def mixer():
        winsem = [newsem() for _ in range(3)]
        items = [("p", m) for m in range(8)] + [("k", m) for m in range(8)] + [("v", m) for m in range(8)] \
            + [("q", m) for m in range(8)] + [("o", c) for c in range(16)]
        colbase = {"p": 0, "q": 8, "k": 16, "v": 24}

        def load_item(i):
            kind, m = items[i]
            sl = i % 3
            src = wout_d[m] if kind == "o" else win_d[colbase[kind] + m]
            S.dma("pool", lambda e, sl=sl, src=src: e.dma_start(out=phb(WIN_OFF[sl], 2048), in_=src),
                  winsem[sl], writes=[("win", sl)])

        for i in range(3):
            load_item(i)
        norm([B0, B1, BE], 16)

        kT = phb(KT_OFF, 8 * 1040)
        Vt = phb(V_OFF, 9 * 1024)
        qT = phb(QT_OFF, 8 * 1024)
        pooled = phb(MISC_OFF, 8192)
        smf = phf(SM_OFF + 1024, 704)
        cqT = phb(SM_OFF, 1024)
        L_own = smf[:, 0:72]
        L_oth = smf[:, 72:136]
        cumO = smf[:, 136:200]
        kb_oth = smf[:, 200:264]
        xtot = smf[:, 264:272]
        mtot = smf[:, 272:280]
        kb_meta = smf[:, 280:288]
        ftmp = smf[:, 288:360]
        t1 = smf[:, 360:368]
        pbuf = [phf(QT_OFF + i * 2080, 1040) for i in range(4)]

        def proj(i, blocks):
            sl = i % 3
            w = phb(WIN_OFF[sl], 2048)
            banks = {b["name"]: acq() for b in blocks}
            for k in range(DC):
                lhsT = w[:, k * 128:(k + 1) * 128]
                for b in blocks:
                    bank = banks[b["name"]]
                    mm(pst[bank][:, 0:b["n"]], lhsT, xn_ap(k, b), k == 0, k == DC - 1,
                       [("win", sl), xnk(k, b)], bank, sig=(k == DC - 1))
            return banks

        def qknorm(banks, blocks, gcolumn, dst_fn, scale):
            sqs = {}
            for b in blocks:
                n = b["n"]
                bank = banks[b["name"]]
                sl = sqrot[0] % 3
                sqrot[0] += 1
                sqa = sq[:, sl * 512:sl * 512 + n]
                sqs[b["name"]] = (sl, sqa)
                S.op("act", lambda e, sqa=sqa, pa=pst[bank][:, 0:n]: e.activation(out=sqa, in_=pa, func=AF.Square),
                     reads=[PK(bank)], writes=[("sq", sl)])
            ys = {}
            for b in blocks:
                n = b["n"]
                sl, sqa = sqs[b["name"]]
                if b["name"] == "be":
                    yb = banks["be"]
                    yap = pst[yb][:, 64:64 + n]
                else:
                    yb = acq()
                    yap = pst[yb][:, 0:n]
                ys[b["name"]] = (yb, yap)
                mm(yap, onesb, sqa, True, True, [("sq", sl), "cb"], yb, sig=True)
            for b in blocks:
                n = b["n"]
                yb, yap = ys[b["name"]]
                ra = rstd[:, b["c0"]:b["c0"] + n]
                S.op("act", lambda e, ra=ra, yap=yap: e.activation(
                    out=ra, in_=yap, func=AF.Ln, bias=corec[:, 2:3], scale=1.0 / 128.0),
                    reads=[PK(yb), "corec"], writes=[("rstd", b["name"])])
                if b["name"] != "be":
                    rel(yb)
            for b in blocks:
                n = b["n"]
                ra = rstd[:, b["c0"]:b["c0"] + n]
                if scale is None:
                    S.op("act", lambda e, ra=ra: e.activation(out=ra, in_=ra, func=AF.Exp, scale=-0.5),
                         reads=[("rstd", b["name"])], writes=[("rstd", b["name"])])
                else:
                    S.op("act", lambda e, ra=ra: e.activation(
                        out=ra, in_=ra, func=AF.Exp, bias=corec[:, 3:4], scale=-0.5),
                        reads=[("rstd", b["name"]), "corec"], writes=[("rstd", b["name"])])
            for b in blocks:
                n = b["n"]
                bank = banks[b["name"]]
                ra = rstd[:, b["c0"]:b["c0"] + n]
                dst, dkey = dst_fn(b)
                S.op("dve", lambda e, dst=dst, pa=pst[bank][:, 0:n], g=qkg[:, gcolumn:gcolumn + 1], ra=ra:
                     e.scalar_tensor_tensor(out=dst, in0=pa, scalar=g, in1=ra, op0=ALU.mult, op1=ALU.mult),
                     reads=[PK(bank), ("rstd", b["name"]), "qkg"], writes=[dkey])
                rel(bank)

        ii = [0]
        pend = []

        def flush():
            while pend:
                pend.pop(0)()

        def run_chunk(blocks, post):
            banks = proj(ii[0], blocks)
            if ii[0] + 3 < len(items):
                load_item(ii[0] + 3)
            ii[0] += 1
            flush()
            pend.append(lambda: post(banks))

        def p_post(m, banks):
            g = m // 2
            w = 2 << g
            p0 = pbuf[m % 2]
            pk = ("pb", m % 2)
            S.op("act", lambda e, p0=p0, pa=pst[banks["b0"]][:, 0:512]: e.activation(
                out=p0[:, 16:528], in_=pa, func=AF.Copy), reads=[PK(banks["b0"])], writes=[pk])
            rel(banks["b0"])
            S.op("act", lambda e, p0=p0, pa=pst[banks["b1"]][:, 0:512]: e.activation(
                out=p0[:, 528:1040], in_=pa, func=AF.Copy), reads=[PK(banks["b1"])], writes=[pk])
            rel(banks["b1"])
            S.op("act", lambda e, p0=p0, pa=pst[banks["be"]][:, 16:32]: e.activation(
                out=p0[:, 0:16], in_=pa, func=AF.Copy), reads=[PK(banks["be"])], writes=[pk])
            rel(banks["be"])
            cur, curk = p0, pk
            sh = 1
            lvl = 0
            lo = 0
            while sh < w:
                nxt = pbuf[2 + lvl % 2]
                nk = ("pb", 2 + lvl % 2)
                lo2 = lo + sh
                S.op("pool", lambda e, nxt=nxt, cur=cur, lo2=lo2, sh=sh: e.tensor_tensor(
                    out=nxt[:, lo2:1040], in0=cur[:, lo2:1040], in1=cur[:, lo2 - sh:1040 - sh], op=ALU.add),
                    reads=[curk], writes=[nk])
                cur, curk = nxt, nk
                lo = lo2
                sh *= 2
                lvl += 1
            S.op("dve", lambda e, cur=cur, p0=p0, m=m, w=w: e.scalar_tensor_tensor(
                out=pooled[:, m * 1024:(m + 1) * 1024], in0=cur[:, 16:1040], scalar=1.0 / w, in1=p0[:, 16:1040],
                op0=ALU.mult, op1=ALU.subtract), reads=[curk, pk], writes=[("pooled", m)])

        for m in range(8):
            run_chunk([B0, B1, BE], lambda banks, m=m: p_post(m, banks))

        BEm = dict(BE)
        BEm["n"] = 16

        def k_post(h, banks):
            def kdst(b):
                if b["name"] == "be":
                    return kT[:, h * 1040 + 1024:h * 1040 + 1040], ("kTm", h)
                return kT[:, h * 1040 + b["c0"]:h * 1040 + b["c0"] + 512], ("kT", h, b["name"])

            qknorm(banks, [B0, B1, BEm], 1, kdst, None)

        for h in range(8):
            run_chunk([B0, B1, BE], lambda banks, h=h: k_post(h, banks))

        ptoks = S.retire([("pb", i) for i in range(4)])
        S.inherit([("vts", 0), ("vts", 1)], ptoks)
        vts = [phb(QT_OFF + i * 1056, 1056) for i in range(2)]

        def v_post(h, banks):
            st = vts[h % 2]
            sk = ("vts", h % 2)
            for b in (B0, B1, BE):
                bank = banks[b["name"]]
                S.op("act", lambda e, st=st, b=b, pa=pst[bank][:, 0:b["n"]]: e.activation(
                    out=st[:, b["c0"]:b["c0"] + b["n"]], in_=pa, func=AF.Copy),
                    reads=[PK(bank)], writes=[sk])
                rel(bank)
            tb = acq()
            tv = pst[tb][:, :].bitcast(BF16)
            for i in range(8):
                S.op("pe", lambda e, tv=tv, st=st, i=i: e.transpose(
                    tv[:, i * 128:(i + 1) * 128], st[:, i * 128:(i + 1) * 128], identb),
                    reads=[sk, "cb"], writes=[PK(tb)], sig=(i == 7))
            S.op("dve", lambda e, tv=tv, h=h: e.tensor_copy(
                out=v3(Vt[:, 0:8192], 8)[:, :, h * 128:(h + 1) * 128], in_=v3(tv, 8)),
                reads=[PK(tb)], writes=[("V", h)])
            rel(tb)
            tb2 = acq()
            tv2 = pst[tb2][:, :].bitcast(BF16)
            S.op("pe", lambda e, tv2=tv2, st=st: e.transpose(tv2[0:32, 0:128], st[:, 1024:1056], identb),
                 reads=[sk, "cb"], writes=[PK(tb2)], sig=True)
            S.op("dve", lambda e, tv2=tv2, h=h: e.tensor_copy(
                out=Vt[0:16, 8192 + h * 128:8192 + (h + 1) * 128], in_=tv2[0:16, 0:128]),
                reads=[PK(tb2)], writes=[("Vm", h)])
            rel(tb2)

        RG = [[0, 1], [2, 3], [4, 5], [6, 7]]
        for h in range(8):
            run_chunk([B0, B1, BE], lambda banks, h=h: v_post(h, banks))
            if h == 0:
                S.dma("sp", lambda e: e.dma_start(
                    out=kb_d.ap().rearrange("(h p) t -> p h t", p=128), in_=v3(kT, 8)[:, :, 0:1024]),
                    newsem(), reads=[("kT", hh, bn) for hh in range(8) for bn in ("b0", "b1")], writes=["kb_d"])
            if h == 3:
                S.dma("pool", lambda e: e.collective_compute(
                    "AllGather", ALU.bypass, replica_groups=RG, ins=[kb_d.ap().opt()], outs=[kg_d.ap().opt()]),
                    newsem(), reads=["kb_d"], writes=["kg_d"], inc=1)
        flush()

        fb = acq()
        for i in range(9):
            rows = 128 if i < 8 else NE
            for k in range(DC):
                blkname = "b0" if i < 4 else ("b1" if i < 8 else "be")
                lhsT = xn[:, k * TX + i * 128:k * TX + i * 128 + rows]
                S.op("pe", lambda e, lhsT=lhsT, rows=rows, i=i, k=k: e.matmul(
                    pst[fb][0:rows, i * 8:(i + 1) * 8], lhsT, wf[:, k * 8:(k + 1) * 8],
                    start=(k == 0), stop=(k == DC - 1)),
                    reads=[("xn", k, blkname), "wf"], writes=[PK(fb)], sig=(k == DC - 1 and i == 8))
        for i in range(9):
            S.op("dve", lambda e, i=i: e.tensor_tensor(
                out=ftmp[:, i * 8:(i + 1) * 8], in0=pst[fb][:, i * 8:(i + 1) * 8], in1=bfr[:, :], op=ALU.add),
                reads=[PK(fb), "bfr"], writes=["ftmp"])
        rel(fb)
        S.op("act", lambda e: e.activation(out=ftmp, in_=ftmp, func=AF.Exp, scale=-1.0),
             reads=["ftmp"], writes=["ftmp"])
        S.op("act", lambda e: e.activation(out=L_own, in_=ftmp, func=AF.Ln, bias=onesf[:, 0:1], scale=1.0),
             reads=["ftmp", "onesf"], writes=["L_own"])

        S.dma("sp", lambda e: e.dma_start(
            out=vb_d.ap().rearrange("(i p) c -> p i c", p=128), in_=v3(Vt[:, 0:8192], 8)),
            newsem(), reads=[("V", h) for h in range(8)], writes=["vb_d"])
        S.dma("sp", lambda e: e.dma_start(
            out=lb_d.ap().rearrange("(i p) c -> p i c", p=128), in_=v3(L_own[:, 0:64], 8)),
            newsem(), reads=["L_own"], writes=["lb_d"])

        def gather_vl():
            for src, dst, ks, kd in ((vb_d, vg_d, "vb_d", "vg_d"), (lb_d, lg_d, "lb_d", "lg_d")):
                S.dma("pool", lambda e, src=src, dst=dst: e.collective_compute(
                    "AllGather", ALU.bypass, replica_groups=RG, ins=[src.ap().opt()], outs=[dst.ap().opt()]),
                    newsem(), reads=[ks], writes=[kd], inc=1)
            S.dma("sp", lambda e: e.dma_start(
                out=v3(L_oth, 8), in_=lg_d.ap()[0:1024, :].rearrange("(i p) c -> p i c", p=128)),
                newsem(), reads=["lg_d"], writes=["L_oth"])

        S.inherit([("qT", h, bn) for h in range(8) for bn in ("b0", "b1")],
                  S.retire([("vts", 0), ("vts", 1)]) + ptoks)
        def q_post(h, banks):
            def qdst(b):
                return qT[:, h * 1024 + b["c0"]:h * 1024 + b["c0"] + 512], ("qT", h, b["name"])

            qknorm(banks, [B0, B1], 0, qdst, ATT_SCALE)

        for h in range(8):
            run_chunk([B0, B1], lambda banks, h=h: q_post(h, banks))
            if h == 1:
                gather_vl()
        flush()

        for g in range(4):
            for dc in range(2):
                m = g * 2 + dc
                for b in (B0, B1):
                    bank = acq()
                    for kc in range(2):
                        lhsT = poolw[:, (g * 2 + kc) * 256 + dc * 128:(g * 2 + kc) * 256 + (dc + 1) * 128]
                        rhs = pooled[:, (g * 2 + kc) * 1024 + b["c0"]:(g * 2 + kc) * 1024 + b["c0"] + 512]
                        mm(pst[bank][:, 0:512], lhsT, rhs, kc == 0, kc == 1,
                           ["poolw", ("pooled", g * 2 + kc)], bank, sig=(kc == 1))
                    S.op("act", lambda e, m=m, b=b, pa=pst[bank][:, 0:512]: e.activation(
                        out=xn_ap(m, b), in_=pa, func=AF.Copy, scale=pscale[:, m:m + 1]),
                        reads=[PK(bank), "pscale"], writes=[xnk(m, b)])
                    rel(bank)

        zb = acq()
        Z = pst[zb]
        nmm = [0]

        def zmm(outp, lhsT, rhs, start, stop, reads, last=False):
            S.op("pe", lambda e: e.matmul(outp, lhsT, rhs, start=start, stop=stop),
                 reads=reads, writes=[PK(zb)], sig=last)

        for i in range(8):
            for i2 in range(i):
                zmm(Z[:, i * 8:(i + 1) * 8], onesf[:, :], L_own[:, i2 * 8:(i2 + 1) * 8], i2 == 0, False,
                    ["onesf", "L_own"])
            zmm(Z[:, i * 8:(i + 1) * 8], tri[:, :], L_own[:, i * 8:(i + 1) * 8], i == 0, True, ["tri", "L_own"])
        for i in range(8):
            for i2 in range(i):
                zmm(Z[:, 64 + i * 8:64 + (i + 1) * 8], onesf[:, :], L_oth[:, i2 * 8:(i2 + 1) * 8], i2 == 0, False,
                    ["onesf", "L_oth"])
            zmm(Z[:, 64 + i * 8:64 + (i + 1) * 8], tri[:, :], L_oth[:, i * 8:(i + 1) * 8], i == 0, True,
                ["tri", "L_oth"])
        for i in range(8):
            zmm(Z[:, 128:136], onesf[:, :], L_oth[:, i * 8:(i + 1) * 8], i == 0, i == 7, ["onesf", "L_oth"])
        zmm(Z[0:16, 136:144], tri[0:16, 0:16], L_own[0:16, 64:72], True, True, ["tri", "L_own"])
        zmm(Z[:, 144:152], onesf[0:16, :], L_own[0:16, 64:72], True, True, ["onesf", "L_own"], last=True)
        S.op("dve", lambda e: e.tensor_copy(out=cumO, in_=Z[:, 0:64]), reads=[PK(zb)], writes=["cumO"])
        S.op("dve", lambda e: e.tensor_copy(out=xtot, in_=Z[:, 128:136]), reads=[PK(zb)], writes=["xtot"])
        S.op("dve", lambda e: e.tensor_copy(out=mtot, in_=Z[:, 144:152]), reads=[PK(zb)], writes=["mtot"])
        for i in range(8):
            S.op("dve", lambda e, i=i: e.scalar_tensor_tensor(
                out=kb_oth[:, i * 8:(i + 1) * 8], in0=Z[:, 64 + i * 8:64 + (i + 1) * 8], scalar=corec[:, 0:1],
                in1=xtot, op0=ALU.add, op1=ALU.subtract),
                reads=[PK(zb), "xtot", "corec"], writes=["kb_oth"])
        S.op("dve", lambda e: e.scalar_tensor_tensor(
            out=t1, in0=xtot, scalar=corec[:, 1:2], in1=mtot, op0=ALU.mult, op1=ALU.add),
            reads=["xtot", "mtot", "corec"], writes=["t1"])
        S.op("dve", lambda e: e.tensor_tensor(
            out=kb_meta[0:16, :], in0=Z[0:16, 136:144], in1=t1[0:16, :], op=ALU.subtract),
            reads=[PK(zb), "t1"], writes=["kb_meta"])
        rel(zb)
        S.op("pool", lambda e: e.memset(cqT, 0.0), writes=[("cqT", 0), ("cqT", 1)])
        for half in range(2):
            cbk = acq()
            for i in range(4):
                it = half * 4 + i
                S.op("pe", lambda e, cbk=cbk, i=i, it=it: e.transpose(
                    pst[cbk][0:8, i * 128:(i + 1) * 128], cumO[:, it * 8:(it + 1) * 8], identf[:, :]),
                    reads=["cumO", "identf"], writes=[PK(cbk)], sig=(i == 3))
            S.op("dve", lambda e, cbk=cbk, half=half: e.tensor_scalar(
                out=cqT[0:8, half * 512:(half + 1) * 512], in0=pst[cbk][0:8, 0:512], scalar1=-1.0, scalar2=None,
                op0=ALU.mult), reads=[PK(cbk)], writes=[("cqT", half)])
            rel(cbk)

        atoks = S.retire([("pooled", m) for m in range(8)])
        osl = [phb(MISC_OFF + i * 2048, 2048) for i in range(2)]
        PTs = [phb(MISC_OFF + 4096 + i * 512, 512) for i in range(3)]
        rinvs = [phf(MISC_OFF + 4096 + 1536 + i * 1024, 512) for i in range(2)]
        S.inherit([("osl", 0), ("osl", 1)] + [("PT", i) for i in range(3)] + [("rinv", i) for i in range(2)], atoks)
        osem = [newsem() for _ in range(4)]

        def load_other(h):
            sl = h % 2
            S.dma("sp", lambda e, sl=sl, h=h: e.dma_start(
                out=osl[sl][:, 0:1024], in_=kg_d.ap()[h * 128:(h + 1) * 128, :]),
                osem[sl * 2], reads=["kg_d"], writes=[("osl", sl)])
            S.dma("sp", lambda e, sl=sl, h=h: e.dma_start(
                out=v3(osl[sl][:, 1024:2048], 8),
                in_=vg_d.ap()[0:1024, h * 128:(h + 1) * 128].rearrange("(i p) c -> p i c", p=128)),
                osem[sl * 2 + 1], reads=["vg_d"], writes=[("oslv", sl)])

        load_other(0)
        ptrot = [0]
        rirot = [0]
        work = []
        for h in range(8):
            for s in range(2):
                tiles = [("m", 0)] + [("o", i) for i in range(4 * (s + 1))] + [("x", i) for i in range(8)]
                for ti, tl in enumerate(tiles):
                    work.append((h, s, tl, ti == 0, ti == len(tiles) - 1))
        LA = 2
        orb = {}

        def qk_issue(w):
            h, s, (kind, i), first, last = w
            sl = h % 2
            qb = B0 if s == 0 else B1
            q_ap = qT[:, h * 1024 + s * 512:h * 1024 + (s + 1) * 512]
            qk = ("qT", h, qb["name"])
            sb_ = acq()
            if kind == "m":
                kt_ = 16
                lhsT = kT[:, h * 1040 + 1024:h * 1040 + 1040]
                kr = [("kTm", h)]
            elif kind == "o":
                kt_ = 128
                lhsT = kT[:, h * 1040 + i * 128:h * 1040 + (i + 1) * 128]
                kr = [("kT", h, "b0" if i < 4 else "b1")]
            else:
                kt_ = 128
                lhsT = osl[sl][:, i * 128:(i + 1) * 128]
                kr = [("osl", sl)]
            dg = (kind == "o" and i // 4 == s)
            mm(pst[sb_][0:kt_, :], lhsT, q_ap, True, False, kr + [qk], sb_)
            mm(pst[sb_][0:kt_, :], esel[:, h * 128:h * 128 + kt_], cqT[:, s * 512:(s + 1) * 512],
               False, not dg, ["esel", ("cqT", s)], sb_, sig=not dg)
            if dg:
                mm(pst[sb_][0:kt_, :], identb, diag(i % 4), False, True, ["cb"], sb_, sig=True)
            return sb_, kt_

        def pv_issue(w, sb_, kt_):
            h, s, (kind, i), first, last = w
            sl = h % 2
            qb = B0 if s == 0 else B1
            if first:
                orb[(h, s)] = (acq(), acq())
            ob, rb = orb[(h, s)]
            if kind == "m":
                bias = kb_meta[0:16, h:h + 1]
                bk = "kb_meta"
                vl = Vt[0:16, 8192 + h * 128:8192 + (h + 1) * 128]
                vk = [("Vm", h)]
            elif kind == "o":
                bias = cumO[:, i * 8 + h:i * 8 + h + 1]
                bk = "cumO"
                vl = Vt[:, i * 1024 + h * 128:i * 1024 + (h + 1) * 128]
                vk = [("V", h)]
            else:
                bias = kb_oth[:, i * 8 + h:i * 8 + h + 1]
                bk = "kb_oth"
                vl = osl[sl][:, 1024 + i * 128:1024 + (i + 1) * 128]
                vk = [("oslv", sl)]
            pi = ptrot[0] % 3
            ptrot[0] += 1
            pt = PTs[pi][0:kt_, :]
            S.op("act", lambda e, pt=pt, pa=pst[sb_][0:kt_, :], bias=bias: e.activation(
                out=pt, in_=pa, func=AF.Exp, bias=bias, scale=1.0),
                reads=[PK(sb_), bk], writes=[("PT", pi)])
            rel(sb_)
            mm(pst[ob][:, :], vl, pt, first, last, vk + [("PT", pi)], ob, sig=False)
            mm(pst[rb][:, :], onesb[0:kt_, :], pt, first, last, ["cb", ("PT", pi)], rb, sig=last)
            if last:
                ri = rirot[0] % 2
                rirot[0] += 1
                S.op("dve", lambda e, ri=ri, rb=rb: e.reciprocal(out=rinvs[ri], in_=pst[rb][:, :]),
                     reads=[PK(rb)], writes=[("rinv", ri)])
                rel(rb)
                S.op("dve", lambda e, ri=ri, h=h, qb=qb, ob=ob: e.tensor_tensor(
                    out=xn_ap(8 + h, qb), in0=pst[ob][:, :], in1=rinvs[ri], op=ALU.mult),
                    reads=[PK(ob), ("rinv", ri)], writes=[xnk(8 + h, qb)])
                rel(ob)

        issued = []
        nq = 0
        loaded = {0}
        for wi, w in enumerate(work):
            h = w[0]
            if w[3] and w[1] == 0 and h + 1 < 8 and (h + 1) not in loaded:
                load_other(h + 1)
                loaded.add(h + 1)
            while nq < min(wi + 1 + LA, len(work)):
                issued.append(qk_issue(work[nq]))
                nq += 1
            sb_, kt_ = issued[wi]
            pv_issue(w, sb_, kt_)

        def o_post(c, banks):
            for b in (B0, B1):
                bank = banks[b["name"]]
                hsrc = b["src"](c)
                S.op("dve", lambda e, hsrc=hsrc, pa=pst[bank][:, 0:512]: e.tensor_tensor(
                    out=hsrc, in0=pa, in1=hsrc, op=ALU.add),
                    reads=[PK(bank)] + b["hk"](c), writes=[("hc", c)])
                rel(bank)

        for c in range(16):
            run_chunk([B0, B1], lambda banks, c=c: o_post(c, banks))
        flush()

    if STAGE >= 2:
        mixer()
        S.barrier()
    if STAGE >= 3:
        ffn(1, [B0, B1], 32, final_store=True)
    else:
        for c4 in range(4):
            out_toks.append(S.dma(
                "sp", lambda e, c4=c4: e.dma_start(
                    out=out_d[c4 * 512:(c4 + 1) * 512, :].rearrange("(c p) t -> p c t", p=128),
                    in_=v3(hT[:, c4 * 4 * T:(c4 + 1) * 4 * T], 4)),
                newsem(), reads=[("hc", cc) for cc in range(c4 * 4, c4 * 4 + 4)] + [("h", 0, c4), ("h", 1, c4)]))
    S.barrier()
    S.q["sp"].append(([(t.sem, t.val) for t in out_toks], None, 3, None))

    with nc.Block() as block:
        def runner(name):
            def f(engine):
                for waits, fn, kind, info in S.q[name]:
                    for sem, val in waits:
                        engine.wait_ge(sem, val)
                    if fn is None:
                        continue
                    ins = fn(engine)
                    if kind == 1:
                        ins.then_inc(esem[name], 1)
                    elif kind == 2:
                        if info[1] == 16:
                            ins.then_inc(info[0], 16)
                        else:
                            ins.then_inc(info[0])
            return f

        block.tensor(runner("pe"))
        block.scalar(runner("act"))
        block.vector(runner("dve"))
        block.gpsimd(runner("pool"))
        block.sync(runner("sp"))
    return nc


def _prep_shared(inp):
    f = np.float32
    sh = {}
    g = np.zeros((128, 48), f)
    for i, nm in enumerate(("ffn1_norm", "mix_norm", "ffn2_norm")):
        g[:, i * 16:(i + 1) * 16] = np.asarray(inp[nm], f)[0].reshape(16, 128).T
    sh["gains"] = g
    sh["qkg"] = np.stack([np.asarray(inp["q_norm"], f)[0], np.asarray(inp["k_norm"], f)[0]], axis=1).copy()
    sh["pscale"] = np.ascontiguousarray(np.asarray(inp["pool_scale"], f)[0].reshape(8, 128).T)
    sh["bfr"] = np.ascontiguousarray(np.broadcast_to(np.asarray(inp["b_forget"], f)[0][None, :], (128, 8)))
    sh["identf"] = np.eye(128, dtype=f)
    sh["onesf"] = np.ones((128, 128), f)
    sh["tri"] = np.triu(np.ones((128, 128), f))
    cbf = np.zeros((128, 2304), f)
    cbf[:, 0:128] = np.eye(128, dtype=f)
    cbf[:, 128:256] = 1.0
    kk = np.arange(128)[:, None]
    qq = np.arange(512)[None, :]
    for i in range(4):
        cbf[:, 256 + i * 512:256 + (i + 1) * 512] = np.where(qq >= kk + 128 * i, 0.0, NEG)
    sh["cbf"] = cbf
    es = np.zeros((128, 1024), f)
    for h in range(8):
        es[h, h * 128:(h + 1) * 128] = 1.0
    sh["esel"] = es
    for i, pre in enumerate(("ffn1", "ffn2")):
        wg = np.asarray(inp[pre + "_w_gate"], f)[0].reshape(16, 128, FC, 128).transpose(2, 1, 0, 3)
        wu = np.asarray(inp[pre + "_w_up"], f)[0].reshape(16, 128, FC, 128).transpose(2, 1, 0, 3)
        sh[f"wgu{i + 1}"] = np.ascontiguousarray(np.stack([wg, wu], axis=2)).reshape(FC, 128, 4096)
        wd = np.asarray(inp[pre + "_w_down"], f)[0].reshape(NG, GJ, 128, 8, 256).transpose(0, 3, 2, 1, 4)
        sh[f"wd{i + 1}"] = np.ascontiguousarray(wd).reshape(NG * 8, 128, GJ * 256)
    w_in = np.asarray(inp["w_in"], f)[0]
    sh["win"] = np.ascontiguousarray(
        w_in[:, 0:4096].reshape(16, 128, 32, 128).transpose(2, 1, 0, 3)).reshape(32, 128, 2048)
    sh["wf"] = np.ascontiguousarray(w_in[:, 4096:4104].reshape(16, 128, 8).transpose(1, 0, 2)).reshape(128, 128)
    sh["wout"] = np.ascontiguousarray(
        np.asarray(inp["w_out"], f)[0].reshape(16, 128, 16, 128).transpose(2, 1, 0, 3)).reshape(16, 128, 2048)
    pw = np.asarray(inp["pool_w"], f)[0].reshape(4, 2, 128, 256).transpose(2, 0, 1, 3)
    sh["poolw"] = np.ascontiguousarray(pw).reshape(128, 2048)
    return sh


_NC_CACHE = {}


def kernel(**inputs):
    x = np.asarray(inputs["x"], np.float32)
    meta = np.asarray(inputs["meta_tokens"], np.float32)
    sh = _prep_shared(inputs)
    in_maps = []
    for core in range(8):
        b, r = core // 2, core % 2
        m = dict(sh)
        m["xT"] = np.ascontiguousarray(x[b, r * T:(r + 1) * T, :].T)
        hist = meta if r == 0 else x[b, T - 16:T, :]
        m["xeT"] = np.ascontiguousarray(np.concatenate([meta, hist], axis=0).T)
        cc = np.zeros((128, 4), np.float32)
        cc[:, 0] = NEG if r == 0 else 0.0
        cc[:, 1] = 0.0 if r == 0 else 1.0
        cc[:, 2] = EPS
        cc[:, 3] = np.log(ATT_SCALE)
        m["corec"] = cc
        in_maps.append(m)
    if "nc" not in _NC_CACHE:
        _NC_CACHE["nc"] = build_program()
    nc = _NC_CACHE["nc"]
    res = run_bass_kernel_spmd(nc, in_maps, core_ids=list(range(8)))
    out = np.empty((4, 2048, D), np.float32)
    for core in range(8):
        b, r = core // 2, core % 2
        out[b, r * T:(r + 1) * T, :] = res.results[core]["outT"].T
    return out
```
